# Optimizing a Trainium2 kernel written in Bass

```python
import jax, jax.numpy as jnp
from jax import lax
import numpy as np

D_MODEL = 1024
BATCH = 8
SEQ = 4096
DEPTH = 2

CHUNK = 64
Q_BLOCK = 128
ROPE_THETA = 500000.0
NORM_EPS = 1e-6
N_MIXERS = 2

DA_HEADS = 8
DA_HEAD_DIM = D_MODEL // (2 * DA_HEADS)
DA_ROT = DA_HEAD_DIM // 4

MLA_HEADS = 16
MLA_Q_RANK = 384
MLA_KV_RANK = 256
MLA_NOPE = 64
MLA_ROPE = 32
MLA_V = 64

FFN_HIDDEN = 2816
CONV_WIDTH = 3

N_DA_LAYERS = (DEPTH + N_MIXERS - 1) // N_MIXERS
N_MLA_LAYERS = DEPTH // N_MIXERS

kernel_name = "hybrid_diffattn_mla_convffn"


def rms_norm(x, g):
    xf = x.astype(jnp.float32)
    y = xf * lax.rsqrt(jnp.mean(xf * xf, axis=-1, keepdims=True) + NORM_EPS)
    return (y * g.astype(jnp.float32)).astype(x.dtype)


def rope_tables(seq_len, rot_dim):
    inv = ROPE_THETA ** (-jnp.arange(0, rot_dim, 2, dtype=jnp.float32) / rot_dim)
    ang = jnp.arange(seq_len, dtype=jnp.float32)[:, None] * inv[None, :]
    return jnp.cos(ang), jnp.sin(ang)


def apply_rope(x, cos, sin):
    shape = (x.shape[1],) + (1,) * (x.ndim - 3) + (cos.shape[-1],)
    c = cos.reshape(shape).astype(x.dtype)
    s = sin.reshape(shape).astype(x.dtype)
    x1, x2 = jnp.split(x, 2, axis=-1)
    return jnp.concatenate([x1 * c - x2 * s, x2 * c + x1 * s], axis=-1)


def chunk_mask(blk, seq_len):
    q_pos = blk * Q_BLOCK + jnp.arange(Q_BLOCK)
    k_pos = jnp.arange(seq_len)
    return (k_pos[None, :] // CHUNK) <= (q_pos[:, None] // CHUNK)


def to_blocks(t):
    b, s = t.shape[0], t.shape[1]
    return jnp.moveaxis(t.reshape((b, s // Q_BLOCK, Q_BLOCK) + t.shape[2:]), 1, 0)


def from_blocks(t):
    t = jnp.moveaxis(t, 0, 1)
    return t.reshape((t.shape[0], t.shape[1] * t.shape[2]) + t.shape[3:])


def masked_softmax(s, mask):
    s = jnp.where(mask, s, jnp.finfo(jnp.float32).min)
    return jax.nn.softmax(s, axis=-1)


def diff_attention(h, w_qkv, lq1, lk1, lq2, lk2, subln_g, w_o, lambda_init, cos, sin):
    b, s, _ = h.shape
    qkv = h @ w_qkv
    q, k, v = jnp.split(qkv, 3, axis=-1)
    q = q.reshape(b, s, DA_HEADS, 2, DA_HEAD_DIM)
    k = k.reshape(b, s, DA_HEADS, 2, DA_HEAD_DIM)
    v = v.reshape(b, s, DA_HEADS, 2 * DA_HEAD_DIM)
    q = jnp.concatenate([apply_rope(q[..., :DA_ROT], cos, sin), q[..., DA_ROT:]], axis=-1)
    k = jnp.concatenate([apply_rope(k[..., :DA_ROT], cos, sin), k[..., DA_ROT:]], axis=-1)
    lam = (jnp.exp(jnp.sum(lq1.astype(jnp.float32) * lk1.astype(jnp.float32)))
           - jnp.exp(jnp.sum(lq2.astype(jnp.float32) * lk2.astype(jnp.float32)))
           + lambda_init)
    scale = DA_HEAD_DIM ** -0.5

    def block(args):
        qb, blk = args
        sc = jnp.einsum('bqhcd,bkhcd->bhcqk', qb, k).astype(jnp.float32) * scale
        p = masked_softmax(sc, chunk_mask(blk, s)[None, None, None])
        a = p[:, :, 0] - lam * p[:, :, 1]
        return jnp.einsum('bhqk,bkhe->bqhe', a.astype(v.dtype), v)

    nb = s // Q_BLOCK
    o = from_blocks(lax.map(block, (to_blocks(q), jnp.arange(nb))))
    o = rms_norm(o, subln_g) * (1.0 - lambda_init)
    return o.reshape(b, s, D_MODEL) @ w_o


def mla_attention(h, w_in, q_norm_g, kv_norm_g, w_uq, w_ukv, w_o, cos, sin):
    b, s, _ = h.shape
    proj = h @ w_in
    c_q = rms_norm(proj[..., :MLA_Q_RANK], q_norm_g)
    c_kv = rms_norm(proj[..., MLA_Q_RANK:MLA_Q_RANK + MLA_KV_RANK], kv_norm_g)
    k_pe = apply_rope(proj[..., MLA_Q_RANK + MLA_KV_RANK:], cos, sin)
    q = (c_q @ w_uq).reshape(b, s, MLA_HEADS, MLA_NOPE + MLA_ROPE)
    q_nope = q[..., :MLA_NOPE]
    q_pe = apply_rope(q[..., MLA_NOPE:], cos, sin)
    kv = (c_kv @ w_ukv).reshape(b, s, MLA_HEADS, MLA_NOPE + MLA_V)
    k_nope = kv[..., :MLA_NOPE]
    v = kv[..., MLA_NOPE:]
    scale = (MLA_NOPE + MLA_ROPE) ** -0.5

    def block(args):
        qn, qp, blk = args
        sc = (jnp.einsum('bqhd,bkhd->bhqk', qn, k_nope)
              + jnp.einsum('bqhr,bkr->bhqk', qp, k_pe)).astype(jnp.float32) * scale
        p = masked_softmax(sc, chunk_mask(blk, s)[None, None])
        return jnp.einsum('bhqk,bkhe->bqhe', p.astype(v.dtype), v)

    nb = s // Q_BLOCK
    o = from_blocks(lax.map(block, (to_blocks(q_nope), to_blocks(q_pe), jnp.arange(nb))))
    return o.reshape(b, s, MLA_HEADS * MLA_V) @ w_o


def conv_ffn(h, w_up, conv_w, conv_b, w_down):
    u = h @ w_up
    c = lax.conv_general_dilated(
        u, conv_w[:, None, :].astype(u.dtype), window_strides=(1,),
        padding=[(CONV_WIDTH - 1, 0)], dimension_numbers=('NWC', 'WIO', 'NWC'),
        feature_group_count=u.shape[-1]) + conv_b
    g, val = jnp.split(c, 2, axis=-1)
    return (jax.nn.silu(g) * val) @ w_down


def setup_inputs(seed: int = 0) -> dict:
    key = jax.random.key(seed)
    ks = jax.random.split(key, 24)
    f32 = jnp.float32

    def nrm(k, shape, scale):
        return jax.random.normal(k, shape, f32) * scale

    def gain(k, shape):
        return 1.0 + 0.01 * jax.random.normal(k, shape, f32)

    mla_in = MLA_Q_RANK + MLA_KV_RANK + MLA_ROPE
    return {
        "x": jax.random.normal(ks[0], (BATCH, SEQ, D_MODEL), f32),
        "attn_norm_g": gain(ks[1], (DEPTH, D_MODEL)),
        "ffn_norm_g": gain(ks[2], (DEPTH, D_MODEL)),
        "da_w_qkv": nrm(ks[3], (N_DA_LAYERS, D_MODEL, 3 * D_MODEL), D_MODEL ** -0.5),
        "da_lam_q1": nrm(ks[4], (N_DA_LAYERS, DA_HEAD_DIM), 0.1),
        "da_lam_k1": nrm(ks[5], (N_DA_LAYERS, DA_HEAD_DIM), 0.1),
        "da_lam_q2": nrm(ks[6], (N_DA_LAYERS, DA_HEAD_DIM), 0.1),
        "da_lam_k2": nrm(ks[7], (N_DA_LAYERS, DA_HEAD_DIM), 0.1),
        "da_subln_g": gain(ks[8], (N_DA_LAYERS, 2 * DA_HEAD_DIM)),
        "da_w_o": nrm(ks[9], (N_DA_LAYERS, D_MODEL, D_MODEL), D_MODEL ** -0.5),
        "mla_w_in": nrm(ks[10], (N_MLA_LAYERS, D_MODEL, mla_in), D_MODEL ** -0.5),
        "mla_q_norm_g": gain(ks[11], (N_MLA_LAYERS, MLA_Q_RANK)),
        "mla_kv_norm_g": gain(ks[12], (N_MLA_LAYERS, MLA_KV_RANK)),
        "mla_w_uq": nrm(ks[13], (N_MLA_LAYERS, MLA_Q_RANK, MLA_HEADS * (MLA_NOPE + MLA_ROPE)), MLA_Q_RANK ** -0.5),
        "mla_w_ukv": nrm(ks[14], (N_MLA_LAYERS, MLA_KV_RANK, MLA_HEADS * (MLA_NOPE + MLA_V)), MLA_KV_RANK ** -0.5),
        "mla_w_o": nrm(ks[15], (N_MLA_LAYERS, MLA_HEADS * MLA_V, D_MODEL), (MLA_HEADS * MLA_V) ** -0.5),
        "ffn_w_up": nrm(ks[16], (DEPTH, D_MODEL, 2 * FFN_HIDDEN), D_MODEL ** -0.5),
        "ffn_conv_w": nrm(ks[17], (DEPTH, CONV_WIDTH, 2 * FFN_HIDDEN), CONV_WIDTH ** -0.5),
        "ffn_conv_b": nrm(ks[18], (DEPTH, 2 * FFN_HIDDEN), 0.01),
        "ffn_w_down": nrm(ks[19], (DEPTH, FFN_HIDDEN, D_MODEL), FFN_HIDDEN ** -0.5),
        "final_norm_g": gain(ks[20], (D_MODEL,)),
    }


def reference(x, attn_norm_g, ffn_norm_g, da_w_qkv, da_lam_q1, da_lam_k1, da_lam_q2, da_lam_k2,
              da_subln_g, da_w_o, mla_w_in, mla_q_norm_g, mla_kv_norm_g, mla_w_uq, mla_w_ukv,
              mla_w_o, ffn_w_up, ffn_conv_w, ffn_conv_b, ffn_w_down, final_norm_g):
    s = x.shape[1]
    cos_da, sin_da = rope_tables(s, DA_ROT)
    cos_mla, sin_mla = rope_tables(s, MLA_ROPE)
    for i in range(DEPTH):
        h = rms_norm(x, attn_norm_g[i])
        j = i // N_MIXERS
        if i % N_MIXERS == 0:
            lambda_init = 0.8 - 0.6 * float(np.exp(-0.3 * i))
            x = x + diff_attention(h, da_w_qkv[j], da_lam_q1[j], da_lam_k1[j], da_lam_q2[j],
                                   da_lam_k2[j], da_subln_g[j], da_w_o[j], lambda_init,
                                   cos_da, sin_da)
        else:
            x = x + mla_attention(h, mla_w_in[j], mla_q_norm_g[j], mla_kv_norm_g[j],
                                  mla_w_uq[j], mla_w_ukv[j], mla_w_o[j], cos_mla, sin_mla)
        h = rms_norm(x, ffn_norm_g[i])
        x = x + conv_ffn(h, ffn_w_up[i], ffn_conv_w[i], ffn_conv_b[i], ffn_w_down[i])
    return rms_norm(x, final_norm_g)
```

```python
import contextlib
import numpy as np
import concourse.bass as bass
import concourse.mybir as mybir
from concourse.bass_utils import run_bass_kernel_spmd

F32 = mybir.dt.float32
BF16 = mybir.dt.bfloat16
AF = mybir.ActivationFunctionType
ALU = mybir.AluOpType

S = 4096
D = 1024
NT = S // 128
NB = S // 512
FF = 2816
NJ = FF // 128
EPS = 1e-6
LAMBDA_INIT0 = 0.8 - 0.6 * 1.0


class Buf:
    __slots__ = ("name", "w", "r", "excl")

    def __init__(self, name="", excl=False):
        self.name = name
        self.w = {}
        self.r = {}
        self.excl = excl


class V:
    __slots__ = ("ap", "bufs")

    def __init__(self, ap, bufs):
        self.ap = ap
        self.bufs = bufs

    def __getitem__(self, k):
        return V(self.ap[k], self.bufs)

    def re(self, s, **kw):
        return V(self.ap.rearrange(s, **kw), self.bufs)

    def bitcast(self, dt):
        return V(self.ap.bitcast(dt), self.bufs)

    def bc(self, shape):
        return V(self.ap.broadcast_to(shape), self.bufs)

    def unsq(self, ax):
        return V(self.ap.unsqueeze(ax), self.bufs)


def _ap(x):
    return x.ap if isinstance(x, V) else x


def _bufs(*xs):
    out = []
    for x in xs:
        if isinstance(x, V):
            out.extend(x.bufs)
    return out


class Prog:
    ENGS = ("pe", "act", "dve", "pool", "sp")

    def __init__(self, nc, n_slots=12):
        self.nc = nc
        self.stream = {e: [] for e in self.ENGS}
        self.cnt = {e: 0 for e in self.ENGS}
        self.waited = {}
        self.n_slots = n_slots
        self.slot_cnt = {}
        self.next_slot = {"sp": 0, "pool": 0, "act": 0}
        self.pe_open = False

    def _deps(self, reads, writes):
        deps = []
        for b in reads:
            deps.extend(b.w.items())
            if b.excl:
                deps.extend(b.r.items())
        for b in writes:
            deps.extend(b.w.items())
            deps.extend(b.r.items())
        return deps

    def _emit_waits(self, eng, deps):
        need = {}
        for key, val in deps:
            if need.get(key, 0) < val:
                need[key] = val
        for key, val in need.items():
            if key == ("e", "pe") and eng == "pe":
                continue
            if self.waited.get((eng, key), 0) >= val:
                continue
            self.waited[(eng, key)] = val
            self.stream[eng].append(("wait", key, val))

    def op(self, eng, fn, reads=(), writes=(), signal=True):
        deps = self._deps(reads, writes)
        self._emit_waits(eng, deps)
        if signal:
            self.cnt[eng] += 1
            idx = self.cnt[eng]
        else:
            idx = self.cnt[eng] + 1
        if eng == "pe":
            self.pe_open = not signal
        self.stream[eng].append(("op", fn, signal))
        key = ("e", eng)
        for b in reads:
            b.r[key] = idx
        for b in writes:
            b.w = {key: idx}
            b.r = {}

    def dma(self, q, out, in_):
        reads = _bufs(in_)
        writes = _bufs(out)
        deps = self._deps(reads, writes)
        slot = self.next_slot[q]
        self.next_slot[q] = (slot + 1) % self.n_slots
        key = ("d", q, slot)
        prev = self.slot_cnt.get(key, 0)
        if prev > 0:
            deps.append((key, prev * 16))
        self._emit_waits(q, deps)
        self.slot_cnt[key] = prev + 1
        val = (prev + 1) * 16
        self.stream[q].append(("dma", _ap(out), _ap(in_), key))
        for b in reads:
            b.r[key] = val
        for b in writes:
            b.w[key] = val
            b.r = {}
        return (key, val)

    def barrier(self):
        assert not self.pe_open
        deps = [(("e", e), self.cnt[e]) for e in self.ENGS if self.cnt[e] > 0]
        deps += [(k, c * 16) for k, c in self.slot_cnt.items()]
        for e in self.ENGS:
            self._emit_waits(e, deps)

    def A(self, out, in_, func, scale=None, bias=None, accum=None):
        kw = {}
        if scale is not None:
            kw["scale"] = _ap(scale)
        if bias is not None:
            kw["bias"] = _ap(bias)
        if accum is not None:
            kw["accum_out"] = _ap(accum)
        o, i = _ap(out), _ap(in_)
        self.op("act", lambda e: e.activation(out=o, in_=i, func=func, **kw),
                reads=_bufs(in_, scale, bias), writes=_bufs(out, accum))

    def TT(self, eng, out, a, b, op):
        o, x, y = _ap(out), _ap(a), _ap(b)
        self.op(eng, lambda e: e.tensor_tensor(out=o, in0=x, in1=y, op=op),
                reads=_bufs(a, b), writes=_bufs(out))

    def TS(self, eng, out, a, s1, op0, s2=None, op1=None):
        o, x = _ap(out), _ap(a)
        if op1 is None:
            s2, op1 = 0.0, ALU.add
        self.op(eng, lambda e: e.tensor_scalar(out=o, in0=x, scalar1=_ap(s1), scalar2=_ap(s2),
                                               op0=op0, op1=op1),
                reads=_bufs(a, s1, s2), writes=_bufs(out))

    def STT(self, out, a, s, b, op0, op1):
        o, x, y = _ap(out), _ap(a), _ap(b)
        self.op("dve", lambda e: e.scalar_tensor_tensor(out=o, in0=x, scalar=_ap(s), in1=y, op0=op0, op1=op1),
                reads=_bufs(a, s, b), writes=_bufs(out))

    def CP(self, eng, out, in_):
        if eng == "act":
            return self.A(out, in_, AF.Copy)
        o, i = _ap(out), _ap(in_)
        self.op(eng, lambda e: e.tensor_copy(out=o, in_=i), reads=_bufs(in_), writes=_bufs(out))

    def RECIP(self, out, in_):
        o, i = _ap(out), _ap(in_)
        self.op("dve", lambda e: e.reciprocal(out=o, in_=i), reads=_bufs(in_), writes=_bufs(out))

    def MEMSET(self, eng, out, val):
        o = _ap(out)
        self.op(eng, lambda e: e.memset(o, val), writes=_bufs(out))

    def MM(self, out, lhsT, rhs, start, stop, last):
        o, l, r = _ap(out), _ap(lhsT), _ap(rhs)
        self.op("pe", lambda e: e.matmul(o, lhsT=l, rhs=r, start=start, stop=stop, skip_group_check=True),
                reads=_bufs(lhsT, rhs), writes=_bufs(out), signal=last)

    def TR(self, out, in_, ident, last):
        o, i, d = _ap(out), _ap(in_), _ap(ident)
        self.op("pe", lambda e: e.transpose(out=o, in_=i, identity=d),
                reads=_bufs(in_, ident), writes=_bufs(out), signal=last)

    def emit(self):
        nc = self.nc
        with contextlib.ExitStack() as st:
            sems = {}
            for e in self.ENGS:
                sems[("e", e)] = st.enter_context(nc.semaphore("s_" + e))
            for key in self.slot_cnt:
                sems[key] = st.enter_context(nc.semaphore("d_%s_%d" % (key[1], key[2])))
            block = st.enter_context(nc.Block())
            streams = self.stream

            def run(engname, eng):
                mysem = sems[("e", engname)]
                for item in streams[engname]:
                    if item[0] == "wait":
                        eng.wait_ge(sems[item[1]], item[2])
                    elif item[0] == "op":
                        ins = item[1](eng)
                        if item[2]:
                            ins.then_inc(mysem, 1)
                    else:
                        eng.dma_start(out=item[1], in_=item[2]).then_inc(sems[item[3]], 16)

            @block.tensor
            def _(eng):
                run("pe", eng)

            @block.scalar
            def _(eng):
                run("act", eng)

            @block.vector
            def _(eng):
                run("dve", eng)

            @block.gpsimd
            def _(eng):
                run("pool", eng)

            @block.sync
            def _(eng):
                run("sp", eng)


class SBAlloc:
    def __init__(self, nc, base=16512, limit=229344):
        self.nc = nc
        self.off = base
        self.n = 0
        self.limit = limit

    def tile(self, shape, dtype, name="t"):
        n = 1
        for s in shape[1:]:
            n *= s
        nbytes = n * (4 if dtype == F32 else 2)
        nbytes = (nbytes + 63) // 64 * 64
        self.n += 1
        t = self.nc.alloc_sbuf_tensor_at("%s_%d" % (name, self.n), list(shape), dtype, offset=self.off)
        self.off += nbytes
        assert self.off <= self.limit, ("SBUF overflow", name, self.off)
        return V(t.ap(), [Buf(name)])

    def mark(self):
        return self.off

    def release(self, m):
        self.off = m


class Ctx:
    pass


def build_program(debug=()):
    nc = bass.Bass("TRN2", target_bir_lowering=False)
    c = Ctx()
    c.nc = nc
    c.p = Prog(nc)
    c.sb = SBAlloc(nc)

    def din(name, shape):
        return nc.dram_tensor(name, list(shape), F32, kind="ExternalInput").ap()

    def dscr(name, shape, dt):
        kind = "ExternalOutput" if name in debug else "Internal"
        return nc.dram_tensor(name, list(shape), dt, kind=kind).ap()

    c.x = din("x", [S, D])
    c.g_attn = din("attn_norm_g", [2, D])
    c.g_ffn = din("ffn_norm_g", [2, D])
    c.w_qkv = din("da_w_qkv", [D, 3 * D])
    c.lam = din("da_lam", [4, 64])
    c.subln = din("da_subln_g", [128, 1])
    c.w_o0 = din("da_w_o", [D, D])
    c.w_in = din("mla_w_in", [D, 672])
    c.g_q = din("mla_q_norm_g", [384])
    c.g_kv = din("mla_kv_norm_g", [256])
    c.w_uq = din("mla_w_uq", [384, 1536])
    c.w_ukv = din("mla_w_ukv", [256, 2048])
    c.w_o1 = din("mla_w_o", [D, D])
    c.w_up = din("ffn_w_up", [2, D, 2 * FF])
    c.conv_w = din("ffn_conv_w", [2, 3, 128, 2 * NJ])
    c.conv_b = din("ffn_conv_b", [2, 128, 2 * NJ])
    c.w_dn = din("ffn_w_down", [2, FF, D])
    c.g_fin = din("final_norm_g", [D])
    c.ident = din("ident", [128, 128])
    c.rope_da = din("rope_da", [2, S, 8])
    c.rope_mla = din("rope_mla", [2, S, 16])
    c.out = nc.dram_tensor("out", [S, D], F32, kind="ExternalOutput").ap()

    c.QT0 = dscr("QT0", [8, 128, S], BF16)
    c.KT0 = dscr("KT0", [8, 128, S], BF16)
    c.V0 = dscr("V0", [S, D], BF16)
    c.OT0 = dscr("OT0", [8, 128, S], BF16)
    c.X1 = dscr("X1", [S, D], F32)
    c.X2 = dscr("X2", [S, D], F32)
    c.QT1 = dscr("QT1", [16, 96, S], BF16)
    c.KT1 = dscr("KT1", [16, 96, S], BF16)
    c.V1 = dscr("V1", [S, D], BF16)
    c.OT1 = dscr("OT1", [8, 128, S], BF16)
    c.X3 = dscr("X3", [S, D], F32)

    ps = nc.alloc_psum_tensor("ps", [128, 4096], F32).ap()
    c.psbufs = [Buf("bank%d" % i, excl=True) for i in range(8)]
    c.ps = ps

    def bank(i, n=1):
        return V(ps[:, i * 512:(i + n) * 512], c.psbufs[i:i + n])
    c.bank = bank

    p, sb = c.p, c.sb
    idf = sb.tile([128, 128], F32, "idf")
    c.idb = sb.tile([128, 128], BF16, "idb")
    p.dma("sp", idf, c.ident)
    p.CP("dve", c.idb, idf)
    c.ones = sb.tile([128, 128], BF16, "ones")
    p.MEMSET("pool", c.ones, 1.0)
    c.eps = sb.tile([128, 1], F32, "eps")
    p.MEMSET("pool", c.eps, EPS)

    phases = debug and [d for d in debug if isinstance(d, int)] or None
    stop_after = max(phases) if phases else 99
    only = [d for d in debug if isinstance(d, str) and d.startswith("only:")]
    c.nb_limit = NB
    c.level = 99
    for d in debug:
        if isinstance(d, str) and d.startswith("lv:"):
            c.level = int(d[3:])
        if isinstance(d, str) and d.startswith("var:"):
            c.var = int(d[4:])
        if isinstance(d, str) and d.startswith("sub:"):
            c.sub = int(d[4:])
    for d in debug:
        if isinstance(d, str) and d.startswith("nb:"):
            c.nb_limit = int(d[3:])
    if only:
        which = only[0][5:]
        {"1A": phase_1A, "mla": phase_attn_mla}[which](c)
        p.barrier()
        p.emit()
        return nc

    phase_0A(c)
    p.barrier()
    m0 = sb.mark()
    wup0 = prefetch_wup(c, 0)
    phase_attn_da(c)
    p.barrier()
    wdn0 = prefetch_wdn(c, 0)
    phase_oproj(c, c.OT0, c.w_o0, c.x, c.X1)
    p.barrier()
    phase_ffn(c, 0, c.X1, c.X2, False, wup0, wdn0)
    p.barrier()
    sb.release(m0)
    phase_1A(c)
    p.barrier()
    m1 = sb.mark()
    wup1 = prefetch_wup(c, 1)
    phase_attn_mla(c)
    p.barrier()
    wdn1 = prefetch_wdn(c, 1)
    phase_oproj(c, c.OT1, c.w_o1, c.X2, c.X3)
    p.barrier()
    phase_ffn(c, 1, c.X3, c.out, True, wup1, wdn1)
    p.barrier()
    sb.release(m1)
    p.emit()
    return nc


def load_w(c, dst, src_kpn, nk, q="pool"):
    for k in range(nk):
        c.p.dma(q, dst[:, k, :], src_kpn[k * 128:(k + 1) * 128, :])


def rms_rstd(c, src, n, ss, rstd, junk):
    p = c.p
    p.A(junk, src, AF.Square, accum=ss)
    p.A(rstd, ss, AF.Sqrt, scale=1.0 / n, bias=c.eps)
    p.RECIP(rstd, rstd)


def phase_0A(c):
    p, sb, bank = c.p, c.sb, c.bank
    m = sb.mark()
    wq = sb.tile([128, 8, 3 * D], BF16, "wqkv")
    load_w(c, wq, c.w_qkv, 8)
    g0 = sb.tile([128, D], F32, "g0")
    p.dma("sp", g0, c.g_attn[0].partition_broadcast(128))
    cs = sb.tile([128, NT, 8], F32, "cos")
    sn = sb.tile([128, NT, 8], F32, "sin")
    p.dma("sp", cs, c.rope_da[0].rearrange("(t p) i -> p t i", p=128))
    p.dma("sp", sn, c.rope_da[1].rearrange("(t p) i -> p t i", p=128))
    xt = [sb.tile([128, D], F32, "xt") for _ in range(2)]
    junk = sb.tile([128, D], BF16, "junk")
    ss = [sb.tile([128, 1], F32, "ss") for _ in range(2)]
    rstd = [sb.tile([128, 1], F32, "rstd") for _ in range(2)]
    h = [sb.tile([128, D], BF16, "h") for _ in range(2)]
    hT = [sb.tile([128, 8, 128], BF16, "hT") for _ in range(2)]
    qk = [sb.tile([128, 32, 64], BF16, "qk") for _ in range(2)]
    vt = [sb.tile([128, 4, D], BF16, "vt") for _ in range(2)]
    tmp = [sb.tile([128, 32, 8], F32, "tmp") for _ in range(4)]
    qkT = [sb.tile([128, 16, 512], BF16, "qkT") for _ in range(2)]
    xv = c.x.rearrange("(t p) d -> p t d", p=128)
    p.dma("sp", xt[0], xv[:, 0, :])
    for b in range(NB):
        for tt in range(4):
            t = b * 4 + tt
            i2 = t % 2
            if t + 1 < NT:
                p.dma("sp", xt[1 - i2], xv[:, t + 1, :])
            rms_rstd(c, xt[i2], D, ss[i2], rstd[i2], junk)
            p.STT(h[i2], xt[i2], rstd[i2], g0, ALU.mult, ALU.mult)
            b0 = bank(0).bitcast(BF16)
            for k in range(8):
                p.TR(b0[:, k * 128:(k + 1) * 128], h[i2][:, k * 128:(k + 1) * 128], c.idb, last=(k == 7))
            p.CP("act", hT[i2].re("p k n -> p (k n)"), b0)
            for nb_ in range(6):
                for k in range(8):
                    p.MM(bank(1 + nb_), hT[i2][:, k, :], wq[:, k, nb_ * 512:(nb_ + 1) * 512],
                         start=(k == 0), stop=(k == 7), last=(k == 7))
            qkps = bank(1, 4)
            p.CP("act", qk[i2].re("p a b -> p (a b)"), qkps)
            p.CP("dve", vt[b % 2][:, tt, :], bank(5, 2))
            q3 = qkps.re("p (a b) -> p a b", b=64)
            cb = cs[:, t, :].unsq(1).bc([128, 32, 8])
            sbb = sn[:, t, :].unsq(1).bc([128, 32, 8])
            p.TT("dve", tmp[0], q3[:, :, 0:8], cb, ALU.mult)
            p.TT("dve", tmp[1], q3[:, :, 8:16], sbb, ALU.mult)
            p.TT("dve", tmp[2], q3[:, :, 8:16], cb, ALU.mult)
            p.TT("dve", tmp[3], q3[:, :, 0:8], sbb, ALU.mult)
            p.TT("pool", qk[i2][:, :, 0:8], tmp[0], tmp[1], ALU.subtract)
            p.TT("pool", qk[i2][:, :, 8:16], tmp[2], tmp[3], ALU.add)
            qkf = qk[i2].re("p a b -> p (a b)")
            b7 = bank(7).bitcast(BF16)
            for i in range(8):
                p.TR(b7[:, i * 128:(i + 1) * 128], qkf[:, i * 128:(i + 1) * 128], c.idb, last=(i == 7))
            for i in range(8):
                p.TR(b0[:, i * 128:(i + 1) * 128], qkf[:, (8 + i) * 128:(9 + i) * 128], c.idb, last=(i == 7))
            p.CP("act", qkT[b % 2][:, 0:8, tt * 128:(tt + 1) * 128], b7.re("p (a b) -> p a b", b=128))
            p.CP("dve", qkT[b % 2][:, 8:16, tt * 128:(tt + 1) * 128], b0.re("p (a b) -> p a b", b=128))
        blk = slice(b * 512, (b + 1) * 512)
        p.dma("sp", c.QT0.rearrange("h p s -> p h s")[:, :, blk], qkT[b % 2][:, 0:8, :])
        p.dma("sp", c.KT0.rearrange("h p s -> p h s")[:, :, blk], qkT[b % 2][:, 8:16, :])
        p.dma("sp", c.V0.rearrange("(t p) d -> p t d", p=128)[:, b * 4:(b + 1) * 4, :], vt[b % 2])
    sb.release(m)


def attn_steps(qb):
    out = []
    for kt in range(4 * qb + 4):
        j = kt - 4 * qb
        if j < 0:
            out.append((kt, 0, None))
        else:
            out.append((kt, (2 * j + 1) * 64, 2 * j * 64))
    return out


def phase_attn_da(c):
    p, sb, bank = c.p, c.sb, c.bank
    m = sb.mark()
    lamt = sb.tile([128, 4, 64], F32, "lamt")
    p.dma("sp", lamt, c.lam.partition_broadcast(128))
    lprod = sb.tile([128, 2, 64], F32, "lprod")
    lsum = sb.tile([128, 2], F32, "lsum")
    neglam = sb.tile([128, 1], F32, "neglam")
    p.TT("dve", lprod[:, 0, :], lamt[:, 0, :], lamt[:, 1, :], ALU.mult)
    p.TT("dve", lprod[:, 1, :], lamt[:, 2, :], lamt[:, 3, :], ALU.mult)
    lj = sb.tile([128, 64], F32, "lj")
    p.A(lj, lprod[:, 0, :], AF.Identity, accum=lsum[:, 0:1])
    p.A(lj, lprod[:, 1, :], AF.Identity, accum=lsum[:, 1:2])
    p.A(lsum, lsum, AF.Exp)
    p.STT(neglam, lsum[:, 1:2], -LAMBDA_INIT0, lsum[:, 0:1], ALU.add, ALU.subtract)
    gsub = sb.tile([128, 1], F32, "gsub")
    p.dma("sp", gsub, c.subln)
    p.TS("dve", gsub, gsub, 1.0 - LAMBDA_INIT0, ALU.mult)

    KT = [sb.tile([128, S], BF16, "KT") for _ in range(2)]
    QTp = [[sb.tile([128, S], BF16, "QTp") for _ in range(2)] for _ in range(2)]
    for i_ in range(2):
        for cc_ in range(2):
            p.MEMSET("pool", QTp[i_][cc_], 0.0)
    Vt = [sb.tile([128, NT, 128], BF16, "Vt") for _ in range(2)]
    E = [sb.tile([128, 512], BF16, "E") for _ in range(4)]
    r0 = sb.tile([128, 512], F32, "r0")
    r1 = sb.tile([128, 512], F32, "r1")
    t0 = sb.tile([128, 512], F32, "t0")
    t1 = sb.tile([128, 512], F32, "t1")
    o = sb.tile([128, 512], F32, "o")
    sq = sb.tile([128, 512], BF16, "sq")
    rs = sb.tile([128, 512], F32, "rs")
    onb = [sb.tile([128, 512], BF16, "onb") for _ in range(2)]
    Vd = c.V0.rearrange("(t p) (h e) -> h p t e", p=128, e=128)
    scale = 64 ** -0.5
    step_ctr = [0]

    def load_head(hh):
        i2 = hh % 2
        p.dma("sp", KT[i2], c.KT0[hh])
        p.dma("sp", QTp[i2][0][0:64, :], c.QT0[hh][0:64, :])
        p.dma("sp", QTp[i2][1][64:128, :], c.QT0[hh][64:128, :])
        p.dma("sp", Vt[i2], Vd[hh])

    load_head(0)
    grp = 0
    for hh in range(8):
        if hh + 1 < 8:
            load_head(hh + 1)
        i2 = hh % 2
        for qb in range(NB):
            Ob = [bank(0), bank(1)]
            Sb = [bank(2), bank(3)]
            steps = [(cc, kt, c0, h0) for cc in range(2) for (kt, c0, h0) in attn_steps(qb)]
            n = len(steps)
            LAG = 2
            info = {}

            def score(i):
                cc, kt, c0, h0 = steps[i]
                rows = slice(cc * 64, cc * 64 + 64)
                sidx = step_ctr[0]
                step_ctr[0] += 1
                sc = bank(4 + sidx % 4)
                e = E[sidx % 4]
                info[i] = (sc, e)
                q0 = qb * 512
                p.MM(sc[:, c0:512], KT[i2][:, kt * 128:(kt + 1) * 128], QTp[i2][cc][:, q0 + c0:q0 + 512],
                     start=True, stop=True, last=(h0 is None))
                if h0 is not None:
                    p.MM(sc[0:64, h0:h0 + 64], KT[i2][:, kt * 128:kt * 128 + 64],
                         QTp[i2][cc][:, q0 + h0:q0 + h0 + 64], start=True, stop=True, last=True)
                p.A(e[:, c0:512], sc[:, c0:512], AF.Exp, scale=scale)
                if h0 is not None:
                    p.A(e[0:64, h0:h0 + 64], sc[0:64, h0:h0 + 64], AF.Exp, scale=scale)

            def pv(i):
                cc, kt, c0, h0 = steps[i]
                sc, e = info[i]
                first = (kt == 0)
                lastk = (kt == 4 * qb + 3)
                p.MM(Ob[cc][:, c0:512], Vt[i2][:, kt, :], e[:, c0:512], start=first, stop=False, last=False)
                p.MM(Sb[cc][:, c0:512], c.ones, e[:, c0:512], start=first, stop=False, last=(h0 is None))
                if h0 is not None:
                    p.MM(Ob[cc][:, h0:h0 + 64], Vt[i2][0:64, kt, :], e[0:64, h0:h0 + 64],
                         start=False, stop=lastk, last=False)
                    p.MM(Sb[cc][:, h0:h0 + 64], c.ones[0:64, :], e[0:64, h0:h0 + 64],
                         start=False, stop=lastk, last=True)

            for i in range(n + LAG):
                if i < n:
                    score(i)
                if i - LAG >= 0:
                    pv(i - LAG)
            p.A(r0, Sb[0], AF.Ln)
            p.A(r0, r0, AF.Exp, scale=-1.0)
            p.TT("dve", t0, Ob[0], r0, ALU.mult)
            p.A(r1, Sb[1], AF.Ln)
            p.A(r1, r1, AF.Exp, scale=-1.0)
            p.TT("dve", t1, Ob[1], r1, ALU.mult)
            p.STT(o, t1, neglam, t0, ALU.mult, ALU.add)
            p.TT("pool", sq, o, o, ALU.mult)
            sidx = step_ctr[0]
            step_ctr[0] += 1
            ssb = bank(4 + sidx % 4)
            p.MM(ssb, c.ones, sq, start=True, stop=True, last=True)
            p.A(rs, ssb, AF.Ln, scale=1.0 / 128, bias=c.eps)
            p.A(rs, rs, AF.Exp, scale=-0.5)
            ob = onb[grp % 2]
            grp += 1
            p.STT(ob, o, gsub, rs, ALU.mult, ALU.mult)
            p.dma("sp", c.OT0[hh][:, qb * 512:(qb + 1) * 512], ob)
    sb.release(m)


def prefetch_wo(c, w_o):
    wo = c.sb.tile([128, 8, D], BF16, "wo")
    load_w(c, wo, w_o, 8)
    return wo


def prefetch_wdn(c, layer):
    wdn = c.sb.tile([128, NJ, D], BF16, "wdn")
    load_w(c, wdn, c.w_dn[layer], NJ)
    return wdn


def phase_oproj(c, OT, w_o, Xin, Xout):
    p, sb, bank = c.p, c.sb, c.bank
    m = sb.mark()
    wo = prefetch_wo(c, w_o)
    ot = [sb.tile([128, 8, 512], BF16, "ot") for _ in range(2)]
    xin = [sb.tile([128, D], F32, "xin") for _ in range(2)]
    xo = [sb.tile([128, D], F32, "xo") for _ in range(2)]
    OTv = OT.rearrange("h p s -> p h s")
    xiv = Xin.rearrange("(t p) d -> p t d", p=128)
    xov = Xout.rearrange("(t p) d -> p t d", p=128)
    p.dma("sp", ot[0], OTv[:, :, 0:512])
    for b in range(NB):
        if b + 1 < NB:
            p.dma("sp", ot[(b + 1) % 2], OTv[:, :, (b + 1) * 512:(b + 2) * 512])
        for tt in range(4):
            t = b * 4 + tt
            i2 = t % 2
            p.dma("sp", xin[i2], xiv[:, t, :])
            pb = 2 * i2
            for nb_ in range(2):
                for hh in range(8):
                    p.MM(bank(pb + nb_), ot[b % 2][:, hh, tt * 128:(tt + 1) * 128],
                         wo[:, hh, nb_ * 512:(nb_ + 1) * 512], start=(hh == 0), stop=(hh == 7), last=(hh == 7))
            p.TT("dve", xo[i2], xin[i2], bank(pb, 2), ALU.add)
            p.dma("sp", xov[:, t, :], xo[i2])
    sb.release(m)


def prefetch_wup(c, layer):
    wup = c.sb.tile([128, 8, 2 * FF], BF16, "wup")
    load_w(c, wup, c.w_up[layer], 8)
    return wup


def phase_ffn(c, layer, Xin, Xout, final, wup, wdn):
    p, sb, bank = c.p, c.sb, c.bank
    m = sb.mark()
    TB = 256
    g = sb.tile([128, D], F32, "gffn")
    p.dma("sp", g, c.g_ffn[layer].partition_broadcast(128))
    if final:
        gf = sb.tile([128, D], F32, "gfin")
        p.dma("sp", gf, c.g_fin.partition_broadcast(128))
    cw = sb.tile([128, 3, 2 * NJ], F32, "cw")
    cbias = sb.tile([128, 2 * NJ], F32, "cb")
    for jx in range(3):
        p.dma("sp", cw[:, jx, :], c.conv_w[layer][jx])
    p.dma("sp", cbias, c.conv_b[layer])
    halo = sb.tile([128, 2 * NJ, 2], F32, "halo")
    p.MEMSET("pool", halo, 0.0)
    xt = [sb.tile([128, 2, D], F32, "xt") for _ in range(2)]
    junk = sb.tile([128, D], BF16, "junk")
    ss = sb.tile([128, 1], F32, "ss")
    rstd = sb.tile([128, 1], F32, "rstd")
    h = [sb.tile([128, D], BF16, "h") for _ in range(2)]
    hT = [sb.tile([128, 8, TB], BF16, "hT") for _ in range(2)]
    mT = [sb.tile([128, NJ, TB], BF16, "mT") for _ in range(2)]
    ub2 = [sb.tile([128, 2, TB + 2], F32, "ub2") for _ in range(2)]
    cg = [sb.tile([128, TB], F32, "cg") for _ in range(3)]
    cv = [sb.tile([128, TB], F32, "cv") for _ in range(3)]
    xiv = Xin.rearrange("(t p) d -> p t d", p=128)
    xov = Xout.rearrange("(t p) d -> p t d", p=128)
    nblk = S // TB
    p.dma("sp", xt[0], xiv[:, 0:2, :])
    uctr = 0
    for b in range(nblk):
        if b + 1 < nblk:
            p.dma("sp", xt[(b + 1) % 2], xiv[:, (b + 1) * 2:(b + 2) * 2, :])
        xb = xt[b % 2]
        hTb = hT[b % 2]
        for tt in range(2):
            i2 = tt
            rms_rstd(c, xb[:, tt, :], D, ss, rstd, junk)
            p.STT(h[i2], xb[:, tt, :], rstd, g, ALU.mult, ALU.mult)
            b0 = bank(0).bitcast(BF16)
            for k in range(8):
                p.TR(b0[:, k * 128:(k + 1) * 128], h[i2][:, k * 128:(k + 1) * 128], c.idb, last=(k == 7))
            p.CP("act", hTb[:, :, tt * 128:(tt + 1) * 128], b0.re("p (a b) -> p a b", b=128))
        mTb = mT[b % 2]
        pend = None
        for j in range(NJ):
            res = []
            pbase = 1 + (2 * j) % 4
            up = ub2[j % 2]
            for which in range(2):
                ch = which * NJ + j
                pb = bank(pbase + which)
                for k in range(8):
                    p.MM(pb[:, 0:TB], wup[:, k, ch * 128:(ch + 1) * 128], hTb[:, k, :],
                         start=(k == 0), stop=(k == 7), last=(k == 7))
                dst = (cg if which == 0 else cv)[j % 3]
                p.A(dst, pb[:, 0:TB], AF.Identity, scale=cw[:, 2, ch:ch + 1], bias=cbias[:, ch:ch + 1])
                res.append(dst)
            p.CP("act", up[:, :, 2:TB + 2], bank(pbase, 2).re("p (a b) -> p a b", b=512)[:, :, 0:TB])
            for which in range(2):
                ch = which * NJ + j
                u = up[:, which, :]
                dst = res[which]
                p.CP("pool", u[:, 0:2], halo[:, ch, :])
                p.CP("pool", halo[:, ch, :], u[:, TB:TB + 2])
                p.STT(dst, u[:, 1:TB + 1], cw[:, 1, ch:ch + 1], dst, ALU.mult, ALU.add)
                p.STT(dst, u[:, 0:TB], cw[:, 0, ch:ch + 1], dst, ALU.mult, ALU.add)
            if pend is not None:
                p.A(pend[0], pend[0], AF.Silu)
                p.TT("pool", pend[2], pend[0], pend[1], ALU.mult)
            pend = (res[0], res[1], mTb[:, j, :])
        p.A(pend[0], pend[0], AF.Silu)
        p.TT("pool", pend[2], pend[0], pend[1], ALU.mult)
        for tt in range(2):
            t = b * 2 + tt
            i2 = t % 2
            for nb_ in range(2):
                for j in range(NJ):
                    p.MM(bank(5 + nb_), mTb[:, j, tt * 128:(tt + 1) * 128], wdn[:, j, nb_ * 512:(nb_ + 1) * 512],
                         start=(j == 0), stop=(j == NJ - 1), last=(j == NJ - 1))
            xo_ = xb[:, tt, :]
            p.TT("dve", xo_, xo_, bank(5, 2), ALU.add)
            if final:
                rms_rstd(c, xo_, D, ss, rstd, junk)
                p.STT(xo_, xo_, rstd, gf, ALU.mult, ALU.mult)
            p.dma("sp", xov[:, t, :], xo_)
    sb.release(m)


def phase_1A(c):
    p, sb, bank = c.p, c.sb, c.bank
    m = sb.mark()
    win = sb.tile([128, 8, 672], BF16, "win")
    load_w(c, win, c.w_in, 8)
    wuq = sb.tile([128, 3, 1536], BF16, "wuq")
    load_w(c, wuq, c.w_uq, 3)
    wukv = sb.tile([128, 2, 2048], BF16, "wukv")
    load_w(c, wukv, c.w_ukv, 2)
    g1 = sb.tile([128, D], F32, "g1")
    p.dma("sp", g1, c.g_attn[1].partition_broadcast(128))
    gq = sb.tile([128, 384], F32, "gq")
    p.dma("sp", gq, c.g_q.partition_broadcast(128))
    gkv = sb.tile([128, 256], F32, "gkv")
    p.dma("sp", gkv, c.g_kv.partition_broadcast(128))
    cs = sb.tile([128, NT, 16], F32, "cos")
    sn = sb.tile([128, NT, 16], F32, "sin")
    p.dma("sp", cs, c.rope_mla[0].rearrange("(t p) i -> p t i", p=128))
    p.dma("sp", sn, c.rope_mla[1].rearrange("(t p) i -> p t i", p=128))
    xt = [sb.tile([128, D], F32, "xt") for _ in range(2)]
    junk = sb.tile([128, D], BF16, "junk")
    ss = sb.tile([128, 1], F32, "ss")
    rstd = sb.tile([128, 1], F32, "rstd")
    ss2 = sb.tile([128, 1], F32, "ss2")
    rstd2 = sb.tile([128, 1], F32, "rstd2")
    ss3 = sb.tile([128, 1], F32, "ss3")
    rstd3 = sb.tile([128, 1], F32, "rstd3")
    h = [sb.tile([128, D], BF16, "h") for _ in range(2)]
    hT = [sb.tile([128, 8, 128], BF16, "hT") for _ in range(2)]
    cqn = sb.tile([128, 384], BF16, "cqn")
    ckvn = sb.tile([128, 256], BF16, "ckvn")
    kslab = [sb.tile([128, 128], BF16, "kslab") for _ in range(2)]
    for ks in kslab:
        p.MEMSET("pool", ks, 0.0)
    kt4 = [sb.tile([128, 16], F32, "kt4") for _ in range(4)]
    cqT = [sb.tile([128, 3, 128], BF16, "cqT") for _ in range(2)]
    ckvT = [sb.tile([128, 2, 512], BF16, "ckvT") for _ in range(2)]
    kpeT = [sb.tile([128, 512], BF16, "kpeT") for _ in range(2)]
    qtok = [sb.tile([128, 16, 128], BF16, "qtok") for _ in range(2)]
    for qq in qtok:
        p.MEMSET("pool", qq, 0.0)
    qt4 = [sb.tile([128, 16, 16], F32, "qt4") for _ in range(4)]
    qT = [sb.tile([128, 16, 512], BF16, "qT") for _ in range(2)]
    vtok = [sb.tile([128, 4, D], BF16, "vtok") for _ in range(2)]
    knb = [sb.tile([128, 512], BF16, "knb") for _ in range(4)]
    xv = c.X2.rearrange("(t p) d -> p t d", p=128)
    kctr = 0
    nblk_ = min(NB, c.nb_limit)
    p.dma("sp", xt[0], xv[:, 0, :])
    for b in range(nblk_):
        bb = b % 2
        for tt in range(4):
            t = b * 4 + tt
            i2 = t % 2
            if t + 1 < 4 * nblk_:
                p.dma("sp", xt[1 - i2], xv[:, t + 1, :])
            rms_rstd(c, xt[i2], D, ss, rstd, junk)
            p.STT(h[i2], xt[i2], rstd, g1, ALU.mult, ALU.mult)
            b0 = bank(0).bitcast(BF16)
            for k in range(8):
                p.TR(b0[:, k * 128:(k + 1) * 128], h[i2][:, k * 128:(k + 1) * 128], c.idb, last=(k == 7))
            p.CP("act", hT[i2].re("p k n -> p (k n)"), b0)
            if c.level >= 2:
                b1, b2 = bank(1), bank(2)
                for k in range(8):
                    p.MM(b1[:, 0:384], hT[i2][:, k, :], win[:, k, 0:384], start=(k == 0), stop=(k == 7), last=(k == 7))
                for k in range(8):
                    p.MM(b2[:, 0:288], hT[i2][:, k, :], win[:, k, 384:672], start=(k == 0), stop=(k == 7), last=(k == 7))
                rms_rstd(c, b1[:, 0:384], 384, ss2, rstd2, junk[:, 0:384])
                p.STT(cqn, b1[:, 0:384], rstd2, gq, ALU.mult, ALU.mult)
                rms_rstd(c, b2[:, 0:256], 256, ss3, rstd3, junk[:, 0:256])
                p.STT(ckvn, b2[:, 0:256], rstd3, gkv, ALU.mult, ALU.mult)
            if c.level >= 3:
                ks = kslab[i2]
                p.TT("dve", kt4[0], b2[:, 256:272], cs[:, t, :], ALU.mult)
                p.TT("dve", kt4[1], b2[:, 272:288], sn[:, t, :], ALU.mult)
                p.TT("dve", kt4[2], b2[:, 272:288], cs[:, t, :], ALU.mult)
                p.TT("dve", kt4[3], b2[:, 256:272], sn[:, t, :], ALU.mult)
                p.TT("pool", ks[:, 64:80], kt4[0], kt4[1], ALU.subtract)
                p.TT("pool", ks[:, 80:96], kt4[2], kt4[3], ALU.add)
            if c.level >= 4:
                b3 = bank(3).bitcast(BF16)
                sub = getattr(c, "sub", 9)
                for k in range(3):
                    p.TR(b3[:, k * 128:(k + 1) * 128], cqn[:, k * 128:(k + 1) * 128], c.idb, last=(sub == 1 and k == 2))
                if sub >= 2:
                    for k in range(2):
                        p.TR(b3[:, 384 + k * 128:384 + (k + 1) * 128], ckvn[:, k * 128:(k + 1) * 128], c.idb,
                             last=(sub == 2 and k == 1))
                if sub >= 3:
                    p.TR(b3[:, 640:768], ks, c.idb, last=True)
                p.CP("act", cqT[i2].re("p k n -> p (k n)"), b3[:, 0:384])
                var = getattr(c, "var", 0)
                if sub >= 2 and var != 1:
                    p.CP("act", ckvT[bb][:, :, tt * 128:(tt + 1) * 128], b3[:, 384:640].re("p (a b) -> p a b", b=128))
                if sub >= 3:
                    p.CP("dve", kpeT[bb][:, tt * 128:(tt + 1) * 128], b3[:, 640:768])
            if c.level >= 5:
                for nb_ in range(3):
                    for k in range(3):
                        p.MM(bank(4 + nb_), cqT[i2][:, k, :], wuq[:, k, nb_ * 512:(nb_ + 1) * 512],
                             start=(k == 0), stop=(k == 2), last=(k == 2))
                qps = bank(4, 3)
                p.CP("act", qtok[i2][:, :, 0:96], qps.re("p (a b) -> p a b", b=96))
                q3 = qps.re("p (a b) -> p a b", b=96)
                cb = cs[:, t, :].unsq(1).bc([128, 16, 16])
                sbb = sn[:, t, :].unsq(1).bc([128, 16, 16])
                p.TT("dve", qt4[0], q3[:, :, 64:80], cb, ALU.mult)
                p.TT("dve", qt4[1], q3[:, :, 80:96], sbb, ALU.mult)
                p.TT("dve", qt4[2], q3[:, :, 80:96], cb, ALU.mult)
                p.TT("dve", qt4[3], q3[:, :, 64:80], sbb, ALU.mult)
                p.TT("pool", qtok[i2][:, :, 64:80], qt4[0], qt4[1], ALU.subtract)
                p.TT("pool", qtok[i2][:, :, 80:96], qt4[2], qt4[3], ALU.add)
            if c.level >= 6:
                for nb_ in range(2):
                    for k in range(2):
                        p.MM(bank(1 + nb_), ckvT[bb][:, k, tt * 128:(tt + 1) * 128],
                             wukv[:, k, 1024 + nb_ * 512:1024 + (nb_ + 1) * 512],
                             start=(k == 0), stop=(k == 1), last=(k == 1))
                p.CP("act", vtok[bb][:, tt, :], bank(1, 2))
            if c.level >= 7:
                b7 = bank(7).bitcast(BF16)
                for hh in range(8):
                    p.TR(b7[:, hh * 128:(hh + 1) * 128], qtok[i2][:, hh, :], c.idb, last=(hh == 7))
                for hh in range(8):
                    p.TR(b0[:, hh * 128:(hh + 1) * 128], qtok[i2][:, 8 + hh, :], c.idb, last=(hh == 7))
                p.CP("act", qT[bb][:, 0:8, tt * 128:(tt + 1) * 128], b7.re("p (a b) -> p a b", b=128))
                p.CP("dve", qT[bb][:, 8:16, tt * 128:(tt + 1) * 128], b0.re("p (a b) -> p a b", b=128))
        blk = slice(b * 512, (b + 1) * 512)
        if c.level >= 8:
            for pr in range(8):
                pb = bank(3 + pr % 4)
                for k in range(2):
                    p.MM(pb, wukv[:, k, pr * 128:(pr + 1) * 128], ckvT[bb][:, k, :], start=(k == 0), stop=(k == 1),
                         last=(k == 1))
                kn = knb[kctr % 4]
                kctr += 1
                p.CP("dve" if pr % 2 else "act", kn, pb)
                p.dma("sp", c.KT1[2 * pr][0:64, blk], kn[0:64, :])
                p.dma("sp", c.KT1[2 * pr + 1][0:64, blk], kn[64:128, :])
        if c.level >= 9:
            for hh in range(16):
                p.dma("sp", c.KT1[hh][64:96, blk], kpeT[bb][64:96, :])
        if c.level >= 10:
            p.dma("sp", c.QT1.rearrange("h p s -> p h s")[:, :, blk], qT[bb][0:96, :, :])
            p.dma("sp", c.V1.rearrange("(t p) d -> p t d", p=128)[:, b * 4:(b + 1) * 4, :], vtok[bb])
    sb.release(m)


def phase_attn_mla(c):
    p, sb, bank = c.p, c.sb, c.bank
    m = sb.mark()
    KT = [sb.tile([128, S], BF16, "KT") for _ in range(2)]
    QT = [sb.tile([128, S], BF16, "QT") for _ in range(2)]
    VA = [sb.tile([128, NT, 128], BF16, "VA") for _ in range(2)]
    for va in VA:
        p.MEMSET("pool", va[:, :, 64:128], 1.0)
    E = [sb.tile([128, 512], BF16, "E") for _ in range(6)]
    r = sb.tile([128, 512], F32, "r")
    onb = [sb.tile([128, 512], BF16, "onb") for _ in range(2)]
    Vd = c.V1.rearrange("(t p) (h e) -> h p t e", p=128, e=64)
    OTv = c.OT1.rearrange("a (b e) s -> (a b) e s", e=64)
    scale = 96 ** -0.5
    ctr = 0

    def load_head(hh):
        i2 = hh % 2
        p.dma("sp", KT[i2][0:96, :], c.KT1[hh])
        p.dma("sp", QT[i2][0:96, :], c.QT1[hh])
        p.dma("sp", VA[i2][:, :, 0:64], Vd[hh])

    load_head(0)
    grp = 0
    rows = slice(0, 96)
    for hh in range(16):
        if hh + 1 < 16:
            load_head(hh + 1)
        i2 = hh % 2
        for qb in range(NB):
            Ob = bank(grp % 2)
            steps = attn_steps(qb)
            n = len(steps)
            LAG = 3
            info = {}
            q0 = qb * 512

            def score(i):
                nonlocal ctr
                kt, c0, h0 = steps[i]
                sc = bank(2 + ctr % 6)
                e = E[ctr % 6]
                ctr += 1
                info[i] = (sc, e)
                p.MM(sc[:, c0:512], KT[i2][rows, kt * 128:(kt + 1) * 128], QT[i2][rows, q0 + c0:q0 + 512],
                     start=True, stop=True, last=(h0 is None))
                if h0 is not None:
                    p.MM(sc[0:64, h0:h0 + 64], KT[i2][rows, kt * 128:kt * 128 + 64],
                         QT[i2][rows, q0 + h0:q0 + h0 + 64], start=True, stop=True, last=True)
                p.A(e[:, c0:512], sc[:, c0:512], AF.Exp, scale=scale)
                if h0 is not None:
                    p.A(e[0:64, h0:h0 + 64], sc[0:64, h0:h0 + 64], AF.Exp, scale=scale)

            def pv(i):
                kt, c0, h0 = steps[i]
                sc, e = info[i]
                first = (kt == 0)
                lastk = (kt == 4 * qb + 3)
                p.MM(Ob[:, c0:512], VA[i2][:, kt, :], e[:, c0:512], start=first, stop=False, last=(h0 is None))
                if h0 is not None:
                    p.MM(Ob[:, h0:h0 + 64], VA[i2][0:64, kt, :], e[0:64, h0:h0 + 64],
                         start=False, stop=lastk, last=True)

            for i in range(n + LAG):
                if i < n:
                    score(i)
                if i - LAG >= 0:
                    pv(i - LAG)
            p.RECIP(r[0:64, :], Ob[64:128, :])
            ob = onb[grp % 2]
            grp += 1
            p.TT("dve", ob[0:64, :], Ob[0:64, :], r[0:64, :], ALU.mult)
            p.dma("sp", OTv[hh][:, qb * 512:(qb + 1) * 512], ob[0:64, :])
    sb.release(m)


def rope_tables(rot):
    inv = (np.float32(500000.0) ** (-np.arange(0, rot, 2, dtype=np.float32) / np.float32(rot))).astype(np.float32)
    ang = (np.arange(S, dtype=np.float32)[:, None] * inv[None, :]).astype(np.float32)
    return np.stack([np.cos(ang), np.sin(ang)]).astype(np.float32)


def make_in_maps(inp):
    f = lambda a: np.ascontiguousarray(np.asarray(a, dtype=np.float32))
    ukv = f(inp["mla_w_ukv"][0]).reshape(256, 16, 128)
    ukv = np.ascontiguousarray(np.concatenate([ukv[:, :, :64].reshape(256, 1024),
                                               ukv[:, :, 64:].reshape(256, 1024)], axis=1))
    shared = {
        "attn_norm_g": f(inp["attn_norm_g"]),
        "ffn_norm_g": f(inp["ffn_norm_g"]),
        "da_w_qkv": f(inp["da_w_qkv"][0]),
        "da_lam": f(np.stack([inp["da_lam_q1"][0], inp["da_lam_k1"][0], inp["da_lam_q2"][0], inp["da_lam_k2"][0]])),
        "da_subln_g": f(inp["da_subln_g"][0]).reshape(128, 1),
        "da_w_o": f(inp["da_w_o"][0]),
        "mla_w_in": f(inp["mla_w_in"][0]),
        "mla_q_norm_g": f(inp["mla_q_norm_g"][0]),
        "mla_kv_norm_g": f(inp["mla_kv_norm_g"][0]),
        "mla_w_uq": f(inp["mla_w_uq"][0]),
        "mla_w_ukv": ukv,
        "mla_w_o": f(inp["mla_w_o"][0]),
        "ffn_w_up": f(inp["ffn_w_up"]),
        "ffn_conv_w": f(np.asarray(inp["ffn_conv_w"]).reshape(2, 3, 2 * NJ, 128).transpose(0, 1, 3, 2)),
        "ffn_conv_b": f(np.asarray(inp["ffn_conv_b"]).reshape(2, 2 * NJ, 128).transpose(0, 2, 1)),
        "ffn_w_down": f(inp["ffn_w_down"]),
        "final_norm_g": f(inp["final_norm_g"]),
        "ident": np.eye(128, dtype=np.float32),
        "rope_da": rope_tables(16),
        "rope_mla": rope_tables(32),
    }
    x = np.asarray(inp["x"], dtype=np.float32)
    maps = []
    for i in range(8):
        d = dict(shared)
        d["x"] = np.ascontiguousarray(x[i])
        maps.append(d)
    return maps


_NC_CACHE = {}


def kernel(**inputs):
    if "nc" not in _NC_CACHE:
        _NC_CACHE["nc"] = build_program()
    nc = _NC_CACHE["nc"]
    res = run_bass_kernel_spmd(nc, make_in_maps(inputs), core_ids=list(range(8)))
    return np.stack([np.asarray(r["out"], dtype=np.float32) for r in res.results], axis=0)
```

```python
import contextlib
import numpy as np
import concourse.bass as bass
import concourse.mybir as mybir
from concourse.bass_utils import run_bass_kernel_spmd

F32 = mybir.dt.float32
BF16 = mybir.dt.bfloat16
AF = mybir.ActivationFunctionType
ALU = mybir.AluOpType

S = 4096
D = 1024
NT = S // 128
NB = S // 512
FF = 2816
NJ = FF // 128
EPS = 1e-6
LAMBDA_INIT0 = 0.8 - 0.6 * 1.0


class Buf:
    __slots__ = ("name", "w", "r", "excl")

    def __init__(self, name="", excl=False):
        self.name = name
        self.w = {}
        self.r = {}
        self.excl = excl


class V:
    __slots__ = ("ap", "bufs")

    def __init__(self, ap, bufs):
        self.ap = ap
        self.bufs = bufs

    def __getitem__(self, k):
        return V(self.ap[k], self.bufs)

    def re(self, s, **kw):
        return V(self.ap.rearrange(s, **kw), self.bufs)

    def bitcast(self, dt):
        return V(self.ap.bitcast(dt), self.bufs)

    def bc(self, shape):
        return V(self.ap.broadcast_to(shape), self.bufs)

    def unsq(self, ax):
        return V(self.ap.unsqueeze(ax), self.bufs)


def _ap(x):
    return x.ap if isinstance(x, V) else x


def _bufs(*xs):
    out = []
    for x in xs:
        if isinstance(x, V):
            out.extend(x.bufs)
    return out


class Prog:
    ENGS = ("pe", "act", "dve", "pool", "sp")

    def __init__(self, nc, n_slots=12):
        self.nc = nc
        self.stream = {e: [] for e in self.ENGS}
        self.cnt = {e: 0 for e in self.ENGS}
        self.waited = {}
        self.n_slots = n_slots
        self.slot_cnt = {}
        self.next_slot = {"sp": 0, "pool": 0, "act": 0}
        self.pe_open = False

    def _deps(self, reads, writes):
        deps = []
        for b in reads:
            deps.extend(b.w.items())
            if b.excl:
                deps.extend(b.r.items())
        for b in writes:
            deps.extend(b.w.items())
            deps.extend(b.r.items())
        return deps

    def _emit_waits(self, eng, deps):
        need = {}
        for key, val in deps:
            if need.get(key, 0) < val:
                need[key] = val
        for key, val in need.items():
            if key == ("e", "pe") and eng == "pe":
                continue
            if self.waited.get((eng, key), 0) >= val:
                continue
            self.waited[(eng, key)] = val
            self.stream[eng].append(("wait", key, val))

    def op(self, eng, fn, reads=(), writes=(), signal=True):
        deps = self._deps(reads, writes)
        self._emit_waits(eng, deps)
        if signal:
            self.cnt[eng] += 1
            idx = self.cnt[eng]
        else:
            idx = self.cnt[eng] + 1
        if eng == "pe":
            self.pe_open = not signal
        self.stream[eng].append(("op", fn, signal))
        key = ("e", eng)
        for b in reads:
            b.r[key] = idx
        for b in writes:
            b.w = {key: idx}
            b.r = {}

    def dma(self, q, out, in_):
        reads = _bufs(in_)
        writes = _bufs(out)
        deps = self._deps(reads, writes)
        slot = self.next_slot[q]
        self.next_slot[q] = (slot + 1) % self.n_slots
        key = ("d", q, slot)
        prev = self.slot_cnt.get(key, 0)
        if prev > 0:
            deps.append((key, prev * 16))
        self._emit_waits(q, deps)
        self.slot_cnt[key] = prev + 1
        val = (prev + 1) * 16
        self.stream[q].append(("dma", _ap(out), _ap(in_), key))
        for b in reads:
            b.r[key] = val
        for b in writes:
            b.w[key] = val
            b.r = {}
        return (key, val)

    def barrier(self):
        assert not self.pe_open
        deps = [(("e", e), self.cnt[e]) for e in self.ENGS if self.cnt[e] > 0]
        deps += [(k, c * 16) for k, c in self.slot_cnt.items()]
        for e in self.ENGS:
            self._emit_waits(e, deps)

    def A(self, out, in_, func, scale=None, bias=None, accum=None):
        kw = {}
        if scale is not None:
            kw["scale"] = _ap(scale)
        if bias is not None:
            kw["bias"] = _ap(bias)
        if accum is not None:
            kw["accum_out"] = _ap(accum)
        o, i = _ap(out), _ap(in_)
        self.op("act", lambda e: e.activation(out=o, in_=i, func=func, **kw),
                reads=_bufs(in_, scale, bias), writes=_bufs(out, accum))

    def TT(self, eng, out, a, b, op):
        o, x, y = _ap(out), _ap(a), _ap(b)
        self.op(eng, lambda e: e.tensor_tensor(out=o, in0=x, in1=y, op=op),
                reads=_bufs(a, b), writes=_bufs(out))

    def TS(self, eng, out, a, s1, op0, s2=None, op1=None):
        o, x = _ap(out), _ap(a)
        if op1 is None:
            s2, op1 = 0.0, ALU.add
        self.op(eng, lambda e: e.tensor_scalar(out=o, in0=x, scalar1=_ap(s1), scalar2=_ap(s2),
                                               op0=op0, op1=op1),
                reads=_bufs(a, s1, s2), writes=_bufs(out))

    def STT(self, out, a, s, b, op0, op1):
        o, x, y = _ap(out), _ap(a), _ap(b)
        self.op("dve", lambda e: e.scalar_tensor_tensor(out=o, in0=x, scalar=_ap(s), in1=y, op0=op0, op1=op1),
                reads=_bufs(a, s, b), writes=_bufs(out))

    def CP(self, eng, out, in_):
        if eng == "act":
            return self.A(out, in_, AF.Copy)
        o, i = _ap(out), _ap(in_)
        self.op(eng, lambda e: e.tensor_copy(out=o, in_=i), reads=_bufs(in_), writes=_bufs(out))

    def RECIP(self, out, in_):
        o, i = _ap(out), _ap(in_)
        self.op("dve", lambda e: e.reciprocal(out=o, in_=i), reads=_bufs(in_), writes=_bufs(out))

    def MEMSET(self, eng, out, val):
        o = _ap(out)
        self.op(eng, lambda e: e.memset(o, val), writes=_bufs(out))

    def MM(self, out, lhsT, rhs, start, stop, last):
        o, l, r = _ap(out), _ap(lhsT), _ap(rhs)
        self.op("pe", lambda e: e.matmul(o, lhsT=l, rhs=r, start=start, stop=stop, skip_group_check=True),
                reads=_bufs(lhsT, rhs), writes=_bufs(out), signal=last)

    def TR(self, out, in_, ident, last):
        o, i, d = _ap(out), _ap(in_), _ap(ident)
        self.op("pe", lambda e: e.transpose(out=o, in_=i, identity=d),
                reads=_bufs(in_, ident), writes=_bufs(out), signal=last)

    def emit(self):
        nc = self.nc
        with contextlib.ExitStack() as st:
            sems = {}
            for e in self.ENGS:
                sems[("e", e)] = st.enter_context(nc.semaphore("s_" + e))
            for key in self.slot_cnt:
                sems[key] = st.enter_context(nc.semaphore("d_%s_%d" % (key[1], key[2])))
            block = st.enter_context(nc.Block())
            streams = self.stream

            def run(engname, eng):
                mysem = sems[("e", engname)]
                for item in streams[engname]:
                    if item[0] == "wait":
                        eng.wait_ge(sems[item[1]], item[2])
                    elif item[0] == "op":
                        ins = item[1](eng)
                        if item[2]:
                            ins.then_inc(mysem, 1)
                    else:
                        eng.dma_start(out=item[1], in_=item[2]).then_inc(sems[item[3]], 16)

            @block.tensor
            def _(eng):
                run("pe", eng)

            @block.scalar
            def _(eng):
                run("act", eng)

            @block.vector
            def _(eng):
                run("dve", eng)

            @block.gpsimd
            def _(eng):
                run("pool", eng)

            @block.sync
            def _(eng):
                run("sp", eng)


class SBAlloc:
    def __init__(self, nc, base=16512, limit=229344):
        self.nc = nc
        self.off = base
        self.n = 0
        self.limit = limit

    def tile(self, shape, dtype, name="t"):
        n = 1
        for s in shape[1:]:
            n *= s
        nbytes = n * (4 if dtype == F32 else 2)
        nbytes = (nbytes + 63) // 64 * 64
        self.n += 1
        t = self.nc.alloc_sbuf_tensor_at("%s_%d" % (name, self.n), list(shape), dtype, offset=self.off)
        self.off += nbytes
        assert self.off <= self.limit, ("SBUF overflow", name, self.off)
        return V(t.ap(), [Buf(name)])

    def mark(self):
        return self.off

    def release(self, m):
        self.off = m


class Ctx:
    pass


def build_program(debug=()):
    nc = bass.Bass("TRN2", target_bir_lowering=False)
    c = Ctx()
    c.nc = nc
    c.p = Prog(nc)
    c.sb = SBAlloc(nc)

    def din(name, shape):
        return nc.dram_tensor(name, list(shape), F32, kind="ExternalInput").ap()

    def dscr(name, shape, dt):
        kind = "ExternalOutput" if name in debug else "Internal"
        return nc.dram_tensor(name, list(shape), dt, kind=kind).ap()

    c.x = din("x", [S, D])
    c.g_attn = din("attn_norm_g", [2, D])
    c.g_ffn = din("ffn_norm_g", [2, D])
    c.w_qkv = din("da_w_qkv", [D, 3 * D])
    c.lam = din("da_lam", [4, 64])
    c.subln = din("da_subln_g", [128, 1])
    c.w_o0 = din("da_w_o", [D, D])
    c.w_in = din("mla_w_in", [D, 672])
    c.g_q = din("mla_q_norm_g", [384])
    c.g_kv = din("mla_kv_norm_g", [256])
    c.w_uq = din("mla_w_uq", [384, 1536])
    c.w_ukv = din("mla_w_ukv", [256, 2048])
    c.w_o1 = din("mla_w_o", [D, D])
    c.w_up = din("ffn_w_up", [2, D, 2 * FF])
    c.conv_w = din("ffn_conv_w", [2, 3, 128, 2 * NJ])
    c.conv_b = din("ffn_conv_b", [2, 128, 2 * NJ])
    c.w_dn = din("ffn_w_down", [2, FF, D])
    c.g_fin = din("final_norm_g", [D])
    c.ident = din("ident", [128, 128])
    c.rope_da = din("rope_da", [2, S, 8])
    c.rope_mla = din("rope_mla", [2, S, 16])
    c.out = nc.dram_tensor("out", [S, D], F32, kind="ExternalOutput").ap()

    c.QT0 = dscr("QT0", [8, 128, S], BF16)
    c.KT0 = dscr("KT0", [8, 128, S], BF16)
    c.V0 = dscr("V0", [S, D], BF16)
    c.OT0 = dscr("OT0", [8, 128, S], BF16)
    c.X1 = dscr("X1", [S, D], F32)
    c.X2 = dscr("X2", [S, D], F32)
    c.QT1 = dscr("QT1", [16, 96, S], BF16)
    c.KT1 = dscr("KT1", [16, 96, S], BF16)
    c.V1 = dscr("V1", [S, D], BF16)
    c.OT1 = dscr("OT1", [8, 128, S], BF16)
    c.X3 = dscr("X3", [S, D], F32)

    ps = nc.alloc_psum_tensor("ps", [128, 4096], F32).ap()
    c.psbufs = [Buf("bank%d" % i, excl=True) for i in range(8)]
    c.ps = ps

    def bank(i, n=1):
        return V(ps[:, i * 512:(i + n) * 512], c.psbufs[i:i + n])
    c.bank = bank

    p, sb = c.p, c.sb
    idf = sb.tile([128, 128], F32, "idf")
    c.idb = sb.tile([128, 128], BF16, "idb")
    p.dma("sp", idf, c.ident)
    p.CP("dve", c.idb, idf)
    c.ones = sb.tile([128, 128], BF16, "ones")
    p.MEMSET("pool", c.ones, 1.0)
    c.eps = sb.tile([128, 1], F32, "eps")
    p.MEMSET("pool", c.eps, EPS)

    phases = debug and [d for d in debug if isinstance(d, int)] or None
    stop_after = max(phases) if phases else 99
    only = [d for d in debug if isinstance(d, str) and d.startswith("only:")]
    c.nb_limit = NB
    c.level = 99
    for d in debug:
        if isinstance(d, str) and d.startswith("lv:"):
            c.level = int(d[3:])
        if isinstance(d, str) and d.startswith("var:"):
            c.var = int(d[4:])
        if isinstance(d, str) and d.startswith("sub:"):
            c.sub = int(d[4:])
    for d in debug:
        if isinstance(d, str) and d.startswith("nb:"):
            c.nb_limit = int(d[3:])
    if only:
        which = only[0][5:]
        {"1A": phase_1A, "mla": phase_attn_mla}[which](c)
        p.barrier()
        p.emit()
        return nc

    phase_0A(c)
    p.barrier()
    m0 = sb.mark()
    wup0 = prefetch_wup(c, 0)
    phase_attn_da(c)
    p.barrier()
    wdn0 = prefetch_wdn(c, 0)
    phase_oproj(c, c.OT0, c.w_o0, c.x, c.X1)
    p.barrier()
    phase_ffn(c, 0, c.X1, c.X2, False, wup0, wdn0)
    p.barrier()
    sb.release(m0)
    phase_1A(c)
    p.barrier()
    m1 = sb.mark()
    wup1 = prefetch_wup(c, 1)
    phase_attn_mla(c)
    p.barrier()
    wdn1 = prefetch_wdn(c, 1)
    phase_oproj(c, c.OT1, c.w_o1, c.X2, c.X3)
    p.barrier()
    phase_ffn(c, 1, c.X3, c.out, True, wup1, wdn1)
    p.barrier()
    sb.release(m1)
    p.emit()
    return nc


def load_w(c, dst, src_kpn, nk, q="pool"):
    for k in range(nk):
        c.p.dma(q, dst[:, k, :], src_kpn[k * 128:(k + 1) * 128, :])


def rms_rstd(c, src, n, ss, rstd, junk):
    p = c.p
    p.A(junk, src, AF.Square, accum=ss)
    p.A(rstd, ss, AF.Sqrt, scale=1.0 / n, bias=c.eps)
    p.RECIP(rstd, rstd)


def phase_0A(c):
    p, sb, bank = c.p, c.sb, c.bank
    m = sb.mark()
    wq = sb.tile([128, 8, 3 * D], BF16, "wqkv")
    load_w(c, wq, c.w_qkv, 8)
    g0 = sb.tile([128, D], F32, "g0")
    p.dma("sp", g0, c.g_attn[0].partition_broadcast(128))
    cs = sb.tile([128, NT, 8], F32, "cos")
    sn = sb.tile([128, NT, 8], F32, "sin")
    p.dma("sp", cs, c.rope_da[0].rearrange("(t p) i -> p t i", p=128))
    p.dma("sp", sn, c.rope_da[1].rearrange("(t p) i -> p t i", p=128))
    xt = [sb.tile([128, D], F32, "xt") for _ in range(2)]
    junk = sb.tile([128, D], BF16, "junk")
    ss = [sb.tile([128, 1], F32, "ss") for _ in range(2)]
    rstd = [sb.tile([128, 1], F32, "rstd") for _ in range(2)]
    h = [sb.tile([128, D], BF16, "h") for _ in range(2)]
    hT = [sb.tile([128, 8, 128], BF16, "hT") for _ in range(2)]
    qk = [sb.tile([128, 32, 64], BF16, "qk") for _ in range(2)]
    vt = [sb.tile([128, 4, D], BF16, "vt") for _ in range(2)]
    tmp = [sb.tile([128, 32, 8], F32, "tmp") for _ in range(4)]
    qkT = [sb.tile([128, 16, 512], BF16, "qkT") for _ in range(2)]
    xv = c.x.rearrange("(t p) d -> p t d", p=128)
    p.dma("sp", xt[0], xv[:, 0, :])
    for b in range(NB):
        for tt in range(4):
            t = b * 4 + tt
            i2 = t % 2
            if t + 1 < NT:
                p.dma("sp", xt[1 - i2], xv[:, t + 1, :])
            rms_rstd(c, xt[i2], D, ss[i2], rstd[i2], junk)
            p.STT(h[i2], xt[i2], rstd[i2], g0, ALU.mult, ALU.mult)
            b0 = bank(0).bitcast(BF16)
            for k in range(8):
                p.TR(b0[:, k * 128:(k + 1) * 128], h[i2][:, k * 128:(k + 1) * 128], c.idb, last=(k == 7))
            p.CP("act", hT[i2].re("p k n -> p (k n)"), b0)
            for nb_ in range(6):
                for k in range(8):
                    p.MM(bank(1 + nb_), hT[i2][:, k, :], wq[:, k, nb_ * 512:(nb_ + 1) * 512],
                         start=(k == 0), stop=(k == 7), last=(k == 7))
            qkps = bank(1, 4)
            p.CP("act", qk[i2].re("p a b -> p (a b)"), qkps)
            p.CP("dve", vt[b % 2][:, tt, :], bank(5, 2))
            q3 = qkps.re("p (a b) -> p a b", b=64)
            cb = cs[:, t, :].unsq(1).bc([128, 32, 8])
            sbb = sn[:, t, :].unsq(1).bc([128, 32, 8])
            p.TT("dve", tmp[0], q3[:, :, 0:8], cb, ALU.mult)
            p.TT("dve", tmp[1], q3[:, :, 8:16], sbb, ALU.mult)
            p.TT("dve", tmp[2], q3[:, :, 8:16], cb, ALU.mult)
            p.TT("dve", tmp[3], q3[:, :, 0:8], sbb, ALU.mult)
            p.TT("pool", qk[i2][:, :, 0:8], tmp[0], tmp[1], ALU.subtract)
            p.TT("pool", qk[i2][:, :, 8:16], tmp[2], tmp[3], ALU.add)
            qkf = qk[i2].re("p a b -> p (a b)")
            b7 = bank(7).bitcast(BF16)
            for i in range(8):
                p.TR(b7[:, i * 128:(i + 1) * 128], qkf[:, i * 128:(i + 1) * 128], c.idb, last=(i == 7))
            for i in range(8):
                p.TR(b0[:, i * 128:(i + 1) * 128], qkf[:, (8 + i) * 128:(9 + i) * 128], c.idb, last=(i == 7))
            p.CP("act", qkT[b % 2][:, 0:8, tt * 128:(tt + 1) * 128], b7.re("p (a b) -> p a b", b=128))
            p.CP("dve", qkT[b % 2][:, 8:16, tt * 128:(tt + 1) * 128], b0.re("p (a b) -> p a b", b=128))
        blk = slice(b * 512, (b + 1) * 512)
        p.dma("sp", c.QT0.rearrange("h p s -> p h s")[:, :, blk], qkT[b % 2][:, 0:8, :])
        p.dma("sp", c.KT0.rearrange("h p s -> p h s")[:, :, blk], qkT[b % 2][:, 8:16, :])
        p.dma("sp", c.V0.rearrange("(t p) d -> p t d", p=128)[:, b * 4:(b + 1) * 4, :], vt[b % 2])
    sb.release(m)


def attn_steps(qb):
    out = []
    for kt in range(4 * qb + 4):
        j = kt - 4 * qb
        if j < 0:
            out.append((kt, 0, None))
        else:
            out.append((kt, (2 * j + 1) * 64, 2 * j * 64))
    return out


def phase_attn_da(c):
    p, sb, bank = c.p, c.sb, c.bank
    m = sb.mark()
    lamt = sb.tile([128, 4, 64], F32, "lamt")
    p.dma("sp", lamt, c.lam.partition_broadcast(128))
    lprod = sb.tile([128, 2, 64], F32, "lprod")
    lsum = sb.tile([128, 2], F32, "lsum")
    neglam = sb.tile([128, 1], F32, "neglam")
    p.TT("dve", lprod[:, 0, :], lamt[:, 0, :], lamt[:, 1, :], ALU.mult)
    p.TT("dve", lprod[:, 1, :], lamt[:, 2, :], lamt[:, 3, :], ALU.mult)
    lj = sb.tile([128, 64], F32, "lj")
    p.A(lj, lprod[:, 0, :], AF.Identity, accum=lsum[:, 0:1])
    p.A(lj, lprod[:, 1, :], AF.Identity, accum=lsum[:, 1:2])
    p.A(lsum, lsum, AF.Exp)
    p.STT(neglam, lsum[:, 1:2], -LAMBDA_INIT0, lsum[:, 0:1], ALU.add, ALU.subtract)
    gsub = sb.tile([128, 1], F32, "gsub")
    p.dma("sp", gsub, c.subln)
    p.TS("dve", gsub, gsub, 1.0 - LAMBDA_INIT0, ALU.mult)

    KT = [sb.tile([128, S], BF16, "KT") for _ in range(2)]
    QTp = [[sb.tile([128, S], BF16, "QTp") for _ in range(2)] for _ in range(2)]
    for i_ in range(2):
        for cc_ in range(2):
            p.MEMSET("pool", QTp[i_][cc_], 0.0)
    Vt = [sb.tile([128, NT, 128], BF16, "Vt") for _ in range(2)]
    E = [sb.tile([128, 512], BF16, "E") for _ in range(4)]
    r0 = sb.tile([128, 512], F32, "r0")
    r1 = sb.tile([128, 512], F32, "r1")
    t0 = sb.tile([128, 512], F32, "t0")
    t1 = sb.tile([128, 512], F32, "t1")
    o = sb.tile([128, 512], F32, "o")
    sq = sb.tile([128, 512], BF16, "sq")
    rs = sb.tile([128, 512], F32, "rs")
    onb = [sb.tile([128, 512], BF16, "onb") for _ in range(2)]
    Vd = c.V0.rearrange("(t p) (h e) -> h p t e", p=128, e=128)
    scale = 64 ** -0.5
    step_ctr = [0]

    def load_head(hh):
        i2 = hh % 2
        p.dma("sp", KT[i2], c.KT0[hh])
        p.dma("sp", QTp[i2][0][0:64, :], c.QT0[hh][0:64, :])
        p.dma("sp", QTp[i2][1][64:128, :], c.QT0[hh][64:128, :])
        p.dma("sp", Vt[i2], Vd[hh])

    load_head(0)
    grp = 0
    for hh in range(8):
        if hh + 1 < 8:
            load_head(hh + 1)
        i2 = hh % 2
        for qb in range(NB):
            Ob = [bank(0), bank(1)]
            Sb = [bank(2), bank(3)]
            steps = [(cc, kt, c0, h0) for cc in range(2) for (kt, c0, h0) in attn_steps(qb)]
            n = len(steps)
            LAG = 3
            info = {}

            def score(i):
                cc, kt, c0, h0 = steps[i]
                rows = slice(cc * 64, cc * 64 + 64)
                sidx = step_ctr[0]
                step_ctr[0] += 1
                sc = bank(4 + sidx % 4)
                e = E[sidx % 4]
                info[i] = (sc, e)
                q0 = qb * 512
                p.MM(sc[:, c0:512], KT[i2][:, kt * 128:(kt + 1) * 128], QTp[i2][cc][:, q0 + c0:q0 + 512],
                     start=True, stop=True, last=(h0 is None))
                if h0 is not None:
                    p.MM(sc[0:64, h0:h0 + 64], KT[i2][:, kt * 128:kt * 128 + 64],
                         QTp[i2][cc][:, q0 + h0:q0 + h0 + 64], start=True, stop=True, last=True)
                p.A(e[:, c0:512], sc[:, c0:512], AF.Exp, scale=scale)
                if h0 is not None:
                    p.A(e[0:64, h0:h0 + 64], sc[0:64, h0:h0 + 64], AF.Exp, scale=scale)

            def pv(i):
                cc, kt, c0, h0 = steps[i]
                sc, e = info[i]
                first = (kt == 0)
                lastk = (kt == 4 * qb + 3)
                p.MM(Ob[cc][:, c0:512], Vt[i2][:, kt, :], e[:, c0:512], start=first, stop=False, last=False)
                p.MM(Sb[cc][:, c0:512], c.ones, e[:, c0:512], start=first, stop=False, last=(h0 is None))
                if h0 is not None:
                    p.MM(Ob[cc][:, h0:h0 + 64], Vt[i2][0:64, kt, :], e[0:64, h0:h0 + 64],
                         start=False, stop=lastk, last=False)
                    p.MM(Sb[cc][:, h0:h0 + 64], c.ones[0:64, :], e[0:64, h0:h0 + 64],
                         start=False, stop=lastk, last=True)

            for i in range(n + LAG):
                if i < n:
                    score(i)
                if i - LAG >= 0:
                    pv(i - LAG)
            p.A(r0, Sb[0], AF.Ln)
            p.A(r0, r0, AF.Exp, scale=-1.0)
            p.TT("dve", t0, Ob[0], r0, ALU.mult)
            p.A(r1, Sb[1], AF.Ln)
            p.A(r1, r1, AF.Exp, scale=-1.0)
            p.TT("dve", t1, Ob[1], r1, ALU.mult)
            p.STT(o, t1, neglam, t0, ALU.mult, ALU.add)
            p.TT("pool", sq, o, o, ALU.mult)
            sidx = step_ctr[0]
            step_ctr[0] += 1
            ssb = bank(4 + sidx % 4)
            p.MM(ssb, c.ones, sq, start=True, stop=True, last=True)
            p.A(rs, ssb, AF.Ln, scale=1.0 / 128, bias=c.eps)
            p.A(rs, rs, AF.Exp, scale=-0.5)
            ob = onb[grp % 2]
            grp += 1
            p.STT(ob, o, gsub, rs, ALU.mult, ALU.mult)
            p.dma("sp", c.OT0[hh][:, qb * 512:(qb + 1) * 512], ob)
    sb.release(m)


def prefetch_wo(c, w_o):
    wo = c.sb.tile([128, 8, D], BF16, "wo")
    load_w(c, wo, w_o, 8)
    return wo


def prefetch_wdn(c, layer):
    wdn = c.sb.tile([128, NJ, D], BF16, "wdn")
    load_w(c, wdn, c.w_dn[layer], NJ)
    return wdn


def phase_oproj(c, OT, w_o, Xin, Xout):
    p, sb, bank = c.p, c.sb, c.bank
    m = sb.mark()
    wo = prefetch_wo(c, w_o)
    ot = [sb.tile([128, 8, 512], BF16, "ot") for _ in range(2)]
    xin = [sb.tile([128, D], F32, "xin") for _ in range(2)]
    xo = [sb.tile([128, D], F32, "xo") for _ in range(2)]
    OTv = OT.rearrange("h p s -> p h s")
    xiv = Xin.rearrange("(t p) d -> p t d", p=128)
    xov = Xout.rearrange("(t p) d -> p t d", p=128)
    p.dma("sp", ot[0], OTv[:, :, 0:512])
    for b in range(NB):
        if b + 1 < NB:
            p.dma("sp", ot[(b + 1) % 2], OTv[:, :, (b + 1) * 512:(b + 2) * 512])
        for tt in range(4):
            t = b * 4 + tt
            i2 = t % 2
            p.dma("sp", xin[i2], xiv[:, t, :])
            pb = 2 * i2
            for nb_ in range(2):
                for hh in range(8):
                    p.MM(bank(pb + nb_), ot[b % 2][:, hh, tt * 128:(tt + 1) * 128],
                         wo[:, hh, nb_ * 512:(nb_ + 1) * 512], start=(hh == 0), stop=(hh == 7), last=(hh == 7))
            p.TT("dve", xo[i2], xin[i2], bank(pb, 2), ALU.add)
            p.dma("sp", xov[:, t, :], xo[i2])
    sb.release(m)


def prefetch_wup(c, layer):
    wup = c.sb.tile([128, 8, 2 * FF], BF16, "wup")
    load_w(c, wup, c.w_up[layer], 8)
    return wup


def phase_ffn(c, layer, Xin, Xout, final, wup, wdn):
    p, sb, bank = c.p, c.sb, c.bank
    m = sb.mark()
    TB = 256
    g = sb.tile([128, D], F32, "gffn")
    p.dma("sp", g, c.g_ffn[layer].partition_broadcast(128))
    if final:
        gf = sb.tile([128, D], F32, "gfin")
        p.dma("sp", gf, c.g_fin.partition_broadcast(128))
    cw = sb.tile([128, 3, 2 * NJ], F32, "cw")
    cbias = sb.tile([128, 2 * NJ], F32, "cb")
    for jx in range(3):
        p.dma("sp", cw[:, jx, :], c.conv_w[layer][jx])
    p.dma("sp", cbias, c.conv_b[layer])
    halo = sb.tile([128, 2 * NJ, 2], F32, "halo")
    p.MEMSET("pool", halo, 0.0)
    xt = [sb.tile([128, 2, D], F32, "xt") for _ in range(2)]
    junk = sb.tile([128, D], BF16, "junk")
    ss = sb.tile([128, 1], F32, "ss")
    rstd = sb.tile([128, 1], F32, "rstd")
    h = [sb.tile([128, D], BF16, "h") for _ in range(2)]
    hT = [sb.tile([128, 8, TB], BF16, "hT") for _ in range(2)]
    mT = [sb.tile([128, NJ, TB], BF16, "mT") for _ in range(2)]
    ub = [sb.tile([128, TB + 2], F32, "ub") for _ in range(4)]
    cg = [sb.tile([128, TB], F32, "cg") for _ in range(3)]
    cv = [sb.tile([128, TB], F32, "cv") for _ in range(3)]
    xiv = Xin.rearrange("(t p) d -> p t d", p=128)
    xov = Xout.rearrange("(t p) d -> p t d", p=128)
    nblk = S // TB
    p.dma("sp", xt[0], xiv[:, 0:2, :])
    uctr = 0
    for b in range(nblk):
        if b + 1 < nblk:
            p.dma("sp", xt[(b + 1) % 2], xiv[:, (b + 1) * 2:(b + 2) * 2, :])
        xb = xt[b % 2]
        hTb = hT[b % 2]
        for tt in range(2):
            i2 = tt
            rms_rstd(c, xb[:, tt, :], D, ss, rstd, junk)
            p.STT(h[i2], xb[:, tt, :], rstd, g, ALU.mult, ALU.mult)
            b0 = bank(0).bitcast(BF16)
            for k in range(8):
                p.TR(b0[:, k * 128:(k + 1) * 128], h[i2][:, k * 128:(k + 1) * 128], c.idb, last=(k == 7))
            p.CP("act", hTb[:, :, tt * 128:(tt + 1) * 128], b0.re("p (a b) -> p a b", b=128))
        mTb = mT[b % 2]
        pend = None
        for j in range(NJ):
            res = []
            for which in range(2):
                ch = which * NJ + j
                pb = bank(1 + (2 * j + which) % 4)
                for k in range(8):
                    p.MM(pb[:, 0:TB], wup[:, k, ch * 128:(ch + 1) * 128], hTb[:, k, :],
                         start=(k == 0), stop=(k == 7), last=(k == 7))
                u = ub[uctr % 4]
                uctr += 1
                dst = (cg if which == 0 else cv)[j % 3]
                p.A(dst, pb[:, 0:TB], AF.Identity, scale=cw[:, 2, ch:ch + 1], bias=cbias[:, ch:ch + 1])
                p.CP("act", u[:, 2:TB + 2], pb[:, 0:TB])
                p.CP("pool", u[:, 0:2], halo[:, ch, :])
                p.CP("pool", halo[:, ch, :], u[:, TB:TB + 2])
                p.STT(dst, u[:, 1:TB + 1], cw[:, 1, ch:ch + 1], dst, ALU.mult, ALU.add)
                p.STT(dst, u[:, 0:TB], cw[:, 0, ch:ch + 1], dst, ALU.mult, ALU.add)
                res.append(dst)
            if pend is not None:
                p.A(pend[0], pend[0], AF.Silu)
                p.TT("pool", pend[2], pend[0], pend[1], ALU.mult)
            pend = (res[0], res[1], mTb[:, j, :])
        p.A(pend[0], pend[0], AF.Silu)
        p.TT("pool", pend[2], pend[0], pend[1], ALU.mult)
        for tt in range(2):
            t = b * 2 + tt
            i2 = t % 2
            for nb_ in range(2):
                for j in range(NJ):
                    p.MM(bank(5 + nb_), mTb[:, j, tt * 128:(tt + 1) * 128], wdn[:, j, nb_ * 512:(nb_ + 1) * 512],
                         start=(j == 0), stop=(j == NJ - 1), last=(j == NJ - 1))
            xo_ = xb[:, tt, :]
            p.TT("dve", xo_, xo_, bank(5, 2), ALU.add)
            if final:
                rms_rstd(c, xo_, D, ss, rstd, junk)
                p.STT(xo_, xo_, rstd, gf, ALU.mult, ALU.mult)
            p.dma("sp", xov[:, t, :], xo_)
    sb.release(m)


def phase_1A(c):
    p, sb, bank = c.p, c.sb, c.bank
    m = sb.mark()
    win = sb.tile([128, 8, 672], BF16, "win")
    load_w(c, win, c.w_in, 8)
    wuq = sb.tile([128, 3, 1536], BF16, "wuq")
    load_w(c, wuq, c.w_uq, 3)
    wukv = sb.tile([128, 2, 2048], BF16, "wukv")
    load_w(c, wukv, c.w_ukv, 2)
    g1 = sb.tile([128, D], F32, "g1")
    p.dma("sp", g1, c.g_attn[1].partition_broadcast(128))
    gq = sb.tile([128, 384], F32, "gq")
    p.dma("sp", gq, c.g_q.partition_broadcast(128))
    gkv = sb.tile([128, 256], F32, "gkv")
    p.dma("sp", gkv, c.g_kv.partition_broadcast(128))
    cs = sb.tile([128, NT, 16], F32, "cos")
    sn = sb.tile([128, NT, 16], F32, "sin")
    p.dma("sp", cs, c.rope_mla[0].rearrange("(t p) i -> p t i", p=128))
    p.dma("sp", sn, c.rope_mla[1].rearrange("(t p) i -> p t i", p=128))
    xt = [sb.tile([128, D], F32, "xt") for _ in range(2)]
    junk = sb.tile([128, D], BF16, "junk")
    ss = sb.tile([128, 1], F32, "ss")
    rstd = sb.tile([128, 1], F32, "rstd")
    ss2 = sb.tile([128, 1], F32, "ss2")
    rstd2 = sb.tile([128, 1], F32, "rstd2")
    ss3 = sb.tile([128, 1], F32, "ss3")
    rstd3 = sb.tile([128, 1], F32, "rstd3")
    h = [sb.tile([128, D], BF16, "h") for _ in range(2)]
    hT = [sb.tile([128, 8, 128], BF16, "hT") for _ in range(2)]
    cqn = sb.tile([128, 384], BF16, "cqn")
    ckvn = sb.tile([128, 256], BF16, "ckvn")
    kslab = [sb.tile([128, 128], BF16, "kslab") for _ in range(2)]
    for ks in kslab:
        p.MEMSET("pool", ks, 0.0)
    kt4 = [sb.tile([128, 16], F32, "kt4") for _ in range(4)]
    cqT = [sb.tile([128, 3, 128], BF16, "cqT") for _ in range(2)]
    ckvT = [sb.tile([128, 2, 512], BF16, "ckvT") for _ in range(2)]
    kpeT = [sb.tile([128, 512], BF16, "kpeT") for _ in range(2)]
    qtok = [sb.tile([128, 16, 128], BF16, "qtok") for _ in range(2)]
    for qq in qtok:
        p.MEMSET("pool", qq, 0.0)
    qt4 = [sb.tile([128, 16, 16], F32, "qt4") for _ in range(4)]
    qT = [sb.tile([128, 16, 512], BF16, "qT") for _ in range(2)]
    vtok = [sb.tile([128, 4, D], BF16, "vtok") for _ in range(2)]
    knb = [sb.tile([128, 512], BF16, "knb") for _ in range(4)]
    xv = c.X2.rearrange("(t p) d -> p t d", p=128)
    kctr = 0
    nblk_ = min(NB, c.nb_limit)
    p.dma("sp", xt[0], xv[:, 0, :])
    for b in range(nblk_):
        bb = b % 2
        for tt in range(4):
            t = b * 4 + tt
            i2 = t % 2
            if t + 1 < 4 * nblk_:
                p.dma("sp", xt[1 - i2], xv[:, t + 1, :])
            rms_rstd(c, xt[i2], D, ss, rstd, junk)
            p.STT(h[i2], xt[i2], rstd, g1, ALU.mult, ALU.mult)
            b0 = bank(0).bitcast(BF16)
            for k in range(8):
                p.TR(b0[:, k * 128:(k + 1) * 128], h[i2][:, k * 128:(k + 1) * 128], c.idb, last=(k == 7))
            p.CP("act", hT[i2].re("p k n -> p (k n)"), b0)
            if c.level >= 2:
                b1, b2 = bank(1), bank(2)
                for k in range(8):
                    p.MM(b1[:, 0:384], hT[i2][:, k, :], win[:, k, 0:384], start=(k == 0), stop=(k == 7), last=(k == 7))
                for k in range(8):
                    p.MM(b2[:, 0:288], hT[i2][:, k, :], win[:, k, 384:672], start=(k == 0), stop=(k == 7), last=(k == 7))
                rms_rstd(c, b1[:, 0:384], 384, ss2, rstd2, junk[:, 0:384])
                p.STT(cqn, b1[:, 0:384], rstd2, gq, ALU.mult, ALU.mult)
                rms_rstd(c, b2[:, 0:256], 256, ss3, rstd3, junk[:, 0:256])
                p.STT(ckvn, b2[:, 0:256], rstd3, gkv, ALU.mult, ALU.mult)
            if c.level >= 3:
                ks = kslab[i2]
                p.TT("dve", kt4[0], b2[:, 256:272], cs[:, t, :], ALU.mult)
                p.TT("dve", kt4[1], b2[:, 272:288], sn[:, t, :], ALU.mult)
                p.TT("dve", kt4[2], b2[:, 272:288], cs[:, t, :], ALU.mult)
                p.TT("dve", kt4[3], b2[:, 256:272], sn[:, t, :], ALU.mult)
                p.TT("pool", ks[:, 64:80], kt4[0], kt4[1], ALU.subtract)
                p.TT("pool", ks[:, 80:96], kt4[2], kt4[3], ALU.add)
            if c.level >= 4:
                b3 = bank(3).bitcast(BF16)
                sub = getattr(c, "sub", 9)
                for k in range(3):
                    p.TR(b3[:, k * 128:(k + 1) * 128], cqn[:, k * 128:(k + 1) * 128], c.idb, last=(sub == 1 and k == 2))
                if sub >= 2:
                    for k in range(2):
                        p.TR(b3[:, 384 + k * 128:384 + (k + 1) * 128], ckvn[:, k * 128:(k + 1) * 128], c.idb,
                             last=(sub == 2 and k == 1))
                if sub >= 3:
                    p.TR(b3[:, 640:768], ks, c.idb, last=True)
                p.CP("act", cqT[i2].re("p k n -> p (k n)"), b3[:, 0:384])
                var = getattr(c, "var", 0)
                if sub >= 2 and var != 1:
                    p.CP("act", ckvT[bb][:, :, tt * 128:(tt + 1) * 128], b3[:, 384:640].re("p (a b) -> p a b", b=128))
                if sub >= 3:
                    p.CP("dve", kpeT[bb][:, tt * 128:(tt + 1) * 128], b3[:, 640:768])
            if c.level >= 5:
                for nb_ in range(3):
                    for k in range(3):
                        p.MM(bank(4 + nb_), cqT[i2][:, k, :], wuq[:, k, nb_ * 512:(nb_ + 1) * 512],
                             start=(k == 0), stop=(k == 2), last=(k == 2))
                qps = bank(4, 3)
                p.CP("act", qtok[i2][:, :, 0:96], qps.re("p (a b) -> p a b", b=96))
                q3 = qps.re("p (a b) -> p a b", b=96)
                cb = cs[:, t, :].unsq(1).bc([128, 16, 16])
                sbb = sn[:, t, :].unsq(1).bc([128, 16, 16])
                p.TT("dve", qt4[0], q3[:, :, 64:80], cb, ALU.mult)
                p.TT("dve", qt4[1], q3[:, :, 80:96], sbb, ALU.mult)
                p.TT("dve", qt4[2], q3[:, :, 80:96], cb, ALU.mult)
                p.TT("dve", qt4[3], q3[:, :, 64:80], sbb, ALU.mult)
                p.TT("pool", qtok[i2][:, :, 64:80], qt4[0], qt4[1], ALU.subtract)
                p.TT("pool", qtok[i2][:, :, 80:96], qt4[2], qt4[3], ALU.add)
            if c.level >= 6:
                for nb_ in range(2):
                    for k in range(2):
                        p.MM(bank(1 + nb_), ckvT[bb][:, k, tt * 128:(tt + 1) * 128],
                             wukv[:, k, 1024 + nb_ * 512:1024 + (nb_ + 1) * 512],
                             start=(k == 0), stop=(k == 1), last=(k == 1))
                p.CP("act", vtok[bb][:, tt, :], bank(1, 2))
            if c.level >= 7:
                b7 = bank(7).bitcast(BF16)
                for hh in range(8):
                    p.TR(b7[:, hh * 128:(hh + 1) * 128], qtok[i2][:, hh, :], c.idb, last=(hh == 7))
                for hh in range(8):
                    p.TR(b0[:, hh * 128:(hh + 1) * 128], qtok[i2][:, 8 + hh, :], c.idb, last=(hh == 7))
                p.CP("act", qT[bb][:, 0:8, tt * 128:(tt + 1) * 128], b7.re("p (a b) -> p a b", b=128))
                p.CP("dve", qT[bb][:, 8:16, tt * 128:(tt + 1) * 128], b0.re("p (a b) -> p a b", b=128))
        blk = slice(b * 512, (b + 1) * 512)
        if c.level >= 8:
            for pr in range(8):
                pb = bank(3 + pr % 4)
                for k in range(2):
                    p.MM(pb, wukv[:, k, pr * 128:(pr + 1) * 128], ckvT[bb][:, k, :], start=(k == 0), stop=(k == 1),
                         last=(k == 1))
                kn = knb[kctr % 4]
                kctr += 1
                p.CP("dve" if pr % 2 else "act", kn, pb)
                p.dma("sp", c.KT1[2 * pr][0:64, blk], kn[0:64, :])
                p.dma("sp", c.KT1[2 * pr + 1][0:64, blk], kn[64:128, :])
        if c.level >= 9:
            for hh in range(16):
                p.dma("sp", c.KT1[hh][64:96, blk], kpeT[bb][64:96, :])
        if c.level >= 10:
            p.dma("sp", c.QT1.rearrange("h p s -> p h s")[:, :, blk], qT[bb][0:96, :, :])
            p.dma("sp", c.V1.rearrange("(t p) d -> p t d", p=128)[:, b * 4:(b + 1) * 4, :], vtok[bb])
    sb.release(m)


def phase_attn_mla(c):
    p, sb, bank = c.p, c.sb, c.bank
    m = sb.mark()
    KT = [sb.tile([128, S], BF16, "KT") for _ in range(2)]
    QT = [sb.tile([128, S], BF16, "QT") for _ in range(2)]
    VA = [sb.tile([128, NT, 128], BF16, "VA") for _ in range(2)]
    for va in VA:
        p.MEMSET("pool", va[:, :, 64:128], 1.0)
    E = [sb.tile([128, 512], BF16, "E") for _ in range(6)]
    r = sb.tile([128, 512], F32, "r")
    onb = [sb.tile([128, 512], BF16, "onb") for _ in range(2)]
    Vd = c.V1.rearrange("(t p) (h e) -> h p t e", p=128, e=64)
    OTv = c.OT1.rearrange("a (b e) s -> (a b) e s", e=64)
    scale = 96 ** -0.5
    ctr = 0

    def load_head(hh):
        i2 = hh % 2
        p.dma("sp", KT[i2][0:96, :], c.KT1[hh])
        p.dma("sp", QT[i2][0:96, :], c.QT1[hh])
        p.dma("sp", VA[i2][:, :, 0:64], Vd[hh])

    load_head(0)
    grp = 0
    rows = slice(0, 96)
    for hh in range(16):
        if hh + 1 < 16:
            load_head(hh + 1)
        i2 = hh % 2
        for qb in range(NB):
            Ob = bank(grp % 2)
            steps = attn_steps(qb)
            n = len(steps)
            LAG = 3
            info = {}
            q0 = qb * 512

            def score(i):
                nonlocal ctr
                kt, c0, h0 = steps[i]
                sc = bank(2 + ctr % 6)
                e = E[ctr % 6]
                ctr += 1
                info[i] = (sc, e)
                p.MM(sc[:, c0:512], KT[i2][rows, kt * 128:(kt + 1) * 128], QT[i2][rows, q0 + c0:q0 + 512],
                     start=True, stop=True, last=(h0 is None))
                if h0 is not None:
                    p.MM(sc[0:64, h0:h0 + 64], KT[i2][rows, kt * 128:kt * 128 + 64],
                         QT[i2][rows, q0 + h0:q0 + h0 + 64], start=True, stop=True, last=True)
                p.A(e[:, c0:512], sc[:, c0:512], AF.Exp, scale=scale)
                if h0 is not None:
                    p.A(e[0:64, h0:h0 + 64], sc[0:64, h0:h0 + 64], AF.Exp, scale=scale)

            def pv(i):
                kt, c0, h0 = steps[i]
                sc, e = info[i]
                first = (kt == 0)
                lastk = (kt == 4 * qb + 3)
                p.MM(Ob[:, c0:512], VA[i2][:, kt, :], e[:, c0:512], start=first, stop=False, last=(h0 is None))
                if h0 is not None:
                    p.MM(Ob[:, h0:h0 + 64], VA[i2][0:64, kt, :], e[0:64, h0:h0 + 64],
                         start=False, stop=lastk, last=True)

            for i in range(n + LAG):
                if i < n:
                    score(i)
                if i - LAG >= 0:
                    pv(i - LAG)
            p.RECIP(r[0:64, :], Ob[64:128, :])
            ob = onb[grp % 2]
            grp += 1
            p.TT("dve", ob[0:64, :], Ob[0:64, :], r[0:64, :], ALU.mult)
            p.dma("sp", OTv[hh][:, qb * 512:(qb + 1) * 512], ob[0:64, :])
    sb.release(m)


def rope_tables(rot):
    inv = (np.float32(500000.0) ** (-np.arange(0, rot, 2, dtype=np.float32) / np.float32(rot))).astype(np.float32)
    ang = (np.arange(S, dtype=np.float32)[:, None] * inv[None, :]).astype(np.float32)
    return np.stack([np.cos(ang), np.sin(ang)]).astype(np.float32)


def make_in_maps(inp):
    f = lambda a: np.ascontiguousarray(np.asarray(a, dtype=np.float32))
    ukv = f(inp["mla_w_ukv"][0]).reshape(256, 16, 128)
    ukv = np.ascontiguousarray(np.concatenate([ukv[:, :, :64].reshape(256, 1024),
                                               ukv[:, :, 64:].reshape(256, 1024)], axis=1))
    shared = {
        "attn_norm_g": f(inp["attn_norm_g"]),
        "ffn_norm_g": f(inp["ffn_norm_g"]),
        "da_w_qkv": f(inp["da_w_qkv"][0]),
        "da_lam": f(np.stack([inp["da_lam_q1"][0], inp["da_lam_k1"][0], inp["da_lam_q2"][0], inp["da_lam_k2"][0]])),
        "da_subln_g": f(inp["da_subln_g"][0]).reshape(128, 1),
        "da_w_o": f(inp["da_w_o"][0]),
        "mla_w_in": f(inp["mla_w_in"][0]),
        "mla_q_norm_g": f(inp["mla_q_norm_g"][0]),
        "mla_kv_norm_g": f(inp["mla_kv_norm_g"][0]),
        "mla_w_uq": f(inp["mla_w_uq"][0]),
        "mla_w_ukv": ukv,
        "mla_w_o": f(inp["mla_w_o"][0]),
        "ffn_w_up": f(inp["ffn_w_up"]),
        "ffn_conv_w": f(np.asarray(inp["ffn_conv_w"]).reshape(2, 3, 2 * NJ, 128).transpose(0, 1, 3, 2)),
        "ffn_conv_b": f(np.asarray(inp["ffn_conv_b"]).reshape(2, 2 * NJ, 128).transpose(0, 2, 1)),
        "ffn_w_down": f(inp["ffn_w_down"]),
        "final_norm_g": f(inp["final_norm_g"]),
        "ident": np.eye(128, dtype=np.float32),
        "rope_da": rope_tables(16),
        "rope_mla": rope_tables(32),
    }
    x = np.asarray(inp["x"], dtype=np.float32)
    maps = []
    for i in range(8):
        d = dict(shared)
        d["x"] = np.ascontiguousarray(x[i])
        maps.append(d)
    return maps


_NC_CACHE = {}


def kernel(**inputs):
    if "nc" not in _NC_CACHE:
        _NC_CACHE["nc"] = build_program()
    nc = _NC_CACHE["nc"]
    res = run_bass_kernel_spmd(nc, make_in_maps(inputs), core_ids=list(range(8)))
    return np.stack([np.asarray(r["out"], dtype=np.float32) for r in res.results], axis=0)
```

```python
import contextlib
import numpy as np
import concourse.bass as bass
import concourse.mybir as mybir
from concourse.bass_utils import run_bass_kernel_spmd

F32 = mybir.dt.float32
BF16 = mybir.dt.bfloat16
AF = mybir.ActivationFunctionType
ALU = mybir.AluOpType

S = 4096
D = 1024
NT = S // 128
NB = S // 512
FF = 2816
NJ = FF // 128
EPS = 1e-6
LAMBDA_INIT0 = 0.8 - 0.6 * 1.0


class Buf:
    __slots__ = ("name", "w", "r", "excl")

    def __init__(self, name="", excl=False):
        self.name = name
        self.w = {}
        self.r = {}
        self.excl = excl


class V:
    __slots__ = ("ap", "bufs")

    def __init__(self, ap, bufs):
        self.ap = ap
        self.bufs = bufs

    def __getitem__(self, k):
        return V(self.ap[k], self.bufs)

    def re(self, s, **kw):
        return V(self.ap.rearrange(s, **kw), self.bufs)

    def bitcast(self, dt):
        return V(self.ap.bitcast(dt), self.bufs)

    def bc(self, shape):
        return V(self.ap.broadcast_to(shape), self.bufs)

    def unsq(self, ax):
        return V(self.ap.unsqueeze(ax), self.bufs)


def _ap(x):
    return x.ap if isinstance(x, V) else x


def _bufs(*xs):
    out = []
    for x in xs:
        if isinstance(x, V):
            out.extend(x.bufs)
    return out


class Prog:
    ENGS = ("pe", "act", "dve", "pool", "sp")

    def __init__(self, nc, n_slots=12):
        self.nc = nc
        self.stream = {e: [] for e in self.ENGS}
        self.cnt = {e: 0 for e in self.ENGS}
        self.waited = {}
        self.n_slots = n_slots
        self.slot_cnt = {}
        self.next_slot = {"sp": 0, "pool": 0, "act": 0}
        self.pe_open = False

    def _deps(self, reads, writes):
        deps = []
        for b in reads:
            deps.extend(b.w.items())
            if b.excl:
                deps.extend(b.r.items())
        for b in writes:
            deps.extend(b.w.items())
            deps.extend(b.r.items())
        return deps

    def _emit_waits(self, eng, deps):
        need = {}
        for key, val in deps:
            if need.get(key, 0) < val:
                need[key] = val
        for key, val in need.items():
            if key == ("e", "pe") and eng == "pe":
                continue
            if self.waited.get((eng, key), 0) >= val:
                continue
            self.waited[(eng, key)] = val
            self.stream[eng].append(("wait", key, val))

    def op(self, eng, fn, reads=(), writes=(), signal=True):
        deps = self._deps(reads, writes)
        self._emit_waits(eng, deps)
        if signal:
            self.cnt[eng] += 1
            idx = self.cnt[eng]
        else:
            idx = self.cnt[eng] + 1
        if eng == "pe":
            self.pe_open = not signal
        self.stream[eng].append(("op", fn, signal))
        key = ("e", eng)
        for b in reads:
            b.r[key] = idx
        for b in writes:
            b.w = {key: idx}
            b.r = {}

    def dma(self, q, out, in_):
        reads = _bufs(in_)
        writes = _bufs(out)
        deps = self._deps(reads, writes)
        slot = self.next_slot[q]
        self.next_slot[q] = (slot + 1) % self.n_slots
        key = ("d", q, slot)
        prev = self.slot_cnt.get(key, 0)
        if prev > 0:
            deps.append((key, prev * 16))
        self._emit_waits(q, deps)
        self.slot_cnt[key] = prev + 1
        val = (prev + 1) * 16
        self.stream[q].append(("dma", _ap(out), _ap(in_), key))
        for b in reads:
            b.r[key] = val
        for b in writes:
            b.w[key] = val
            b.r = {}
        return (key, val)

    def barrier(self):
        assert not self.pe_open
        deps = [(("e", e), self.cnt[e]) for e in self.ENGS if self.cnt[e] > 0]
        deps += [(k, c * 16) for k, c in self.slot_cnt.items()]
        for e in self.ENGS:
            self._emit_waits(e, deps)

    def A(self, out, in_, func, scale=None, bias=None, accum=None):
        kw = {}
        if scale is not None:
            kw["scale"] = _ap(scale)
        if bias is not None:
            kw["bias"] = _ap(bias)
        if accum is not None:
            kw["accum_out"] = _ap(accum)
        o, i = _ap(out), _ap(in_)
        self.op("act", lambda e: e.activation(out=o, in_=i, func=func, **kw),
                reads=_bufs(in_, scale, bias), writes=_bufs(out, accum))

    def TT(self, eng, out, a, b, op):
        o, x, y = _ap(out), _ap(a), _ap(b)
        self.op(eng, lambda e: e.tensor_tensor(out=o, in0=x, in1=y, op=op),
                reads=_bufs(a, b), writes=_bufs(out))

    def TS(self, eng, out, a, s1, op0, s2=None, op1=None):
        o, x = _ap(out), _ap(a)
        if op1 is None:
            s2, op1 = 0.0, ALU.add
        self.op(eng, lambda e: e.tensor_scalar(out=o, in0=x, scalar1=_ap(s1), scalar2=_ap(s2),
                                               op0=op0, op1=op1),
                reads=_bufs(a, s1, s2), writes=_bufs(out))

    def STT(self, out, a, s, b, op0, op1):
        o, x, y = _ap(out), _ap(a), _ap(b)
        self.op("dve", lambda e: e.scalar_tensor_tensor(out=o, in0=x, scalar=_ap(s), in1=y, op0=op0, op1=op1),
                reads=_bufs(a, s, b), writes=_bufs(out))

    def CP(self, eng, out, in_):
        if eng == "act":
            return self.A(out, in_, AF.Copy)
        o, i = _ap(out), _ap(in_)
        self.op(eng, lambda e: e.tensor_copy(out=o, in_=i), reads=_bufs(in_), writes=_bufs(out))

    def RECIP(self, out, in_):
        o, i = _ap(out), _ap(in_)
        self.op("dve", lambda e: e.reciprocal(out=o, in_=i), reads=_bufs(in_), writes=_bufs(out))

    def MEMSET(self, eng, out, val):
        o = _ap(out)
        self.op(eng, lambda e: e.memset(o, val), writes=_bufs(out))

    def MM(self, out, lhsT, rhs, start, stop, last):
        o, l, r = _ap(out), _ap(lhsT), _ap(rhs)
        self.op("pe", lambda e: e.matmul(o, lhsT=l, rhs=r, start=start, stop=stop, skip_group_check=True),
                reads=_bufs(lhsT, rhs), writes=_bufs(out), signal=last)

    def TR(self, out, in_, ident, last):
        o, i, d = _ap(out), _ap(in_), _ap(ident)
        self.op("pe", lambda e: e.transpose(out=o, in_=i, identity=d),
                reads=_bufs(in_, ident), writes=_bufs(out), signal=last)

    def emit(self):
        nc = self.nc
        with contextlib.ExitStack() as st:
            sems = {}
            for e in self.ENGS:
                sems[("e", e)] = st.enter_context(nc.semaphore("s_" + e))
            for key in self.slot_cnt:
                sems[key] = st.enter_context(nc.semaphore("d_%s_%d" % (key[1], key[2])))
            block = st.enter_context(nc.Block())
            streams = self.stream

            def run(engname, eng):
                mysem = sems[("e", engname)]
                for item in streams[engname]:
                    if item[0] == "wait":
                        eng.wait_ge(sems[item[1]], item[2])
                    elif item[0] == "op":
                        ins = item[1](eng)
                        if item[2]:
                            ins.then_inc(mysem, 1)
                    else:
                        eng.dma_start(out=item[1], in_=item[2]).then_inc(sems[item[3]], 16)

            @block.tensor
            def _(eng):
                run("pe", eng)

            @block.scalar
            def _(eng):
                run("act", eng)

            @block.vector
            def _(eng):
                run("dve", eng)

            @block.gpsimd
            def _(eng):
                run("pool", eng)

            @block.sync
            def _(eng):
                run("sp", eng)


class SBAlloc:
    def __init__(self, nc, base=16512, limit=229344):
        self.nc = nc
        self.off = base
        self.n = 0
        self.limit = limit

    def tile(self, shape, dtype, name="t"):
        n = 1
        for s in shape[1:]:
            n *= s
        nbytes = n * (4 if dtype == F32 else 2)
        nbytes = (nbytes + 63) // 64 * 64
        self.n += 1
        t = self.nc.alloc_sbuf_tensor_at("%s_%d" % (name, self.n), list(shape), dtype, offset=self.off)
        self.off += nbytes
        assert self.off <= self.limit, ("SBUF overflow", name, self.off)
        return V(t.ap(), [Buf(name)])

    def mark(self):
        return self.off

    def release(self, m):
        self.off = m


class Ctx:
    pass


def build_program(debug=()):
    nc = bass.Bass("TRN2", target_bir_lowering=False)
    c = Ctx()
    c.nc = nc
    c.p = Prog(nc)
    c.sb = SBAlloc(nc)

    def din(name, shape):
        return nc.dram_tensor(name, list(shape), F32, kind="ExternalInput").ap()

    def dscr(name, shape, dt):
        kind = "ExternalOutput" if name in debug else "Internal"
        return nc.dram_tensor(name, list(shape), dt, kind=kind).ap()

    c.x = din("x", [S, D])
    c.g_attn = din("attn_norm_g", [2, D])
    c.g_ffn = din("ffn_norm_g", [2, D])
    c.w_qkv = din("da_w_qkv", [D, 3 * D])
    c.lam = din("da_lam", [4, 64])
    c.subln = din("da_subln_g", [128, 1])
    c.w_o0 = din("da_w_o", [D, D])
    c.w_in = din("mla_w_in", [D, 672])
    c.g_q = din("mla_q_norm_g", [384])
    c.g_kv = din("mla_kv_norm_g", [256])
    c.w_uq = din("mla_w_uq", [384, 1536])
    c.w_ukv = din("mla_w_ukv", [256, 2048])
    c.w_o1 = din("mla_w_o", [D, D])
    c.w_up = din("ffn_w_up", [2, D, 2 * FF])
    c.conv_w = din("ffn_conv_w", [2, 3, 128, 2 * NJ])
    c.conv_b = din("ffn_conv_b", [2, 128, 2 * NJ])
    c.w_dn = din("ffn_w_down", [2, FF, D])
    c.g_fin = din("final_norm_g", [D])
    c.ident = din("ident", [128, 128])
    c.rope_da = din("rope_da", [2, S, 8])
    c.rope_mla = din("rope_mla", [2, S, 16])
    c.out = nc.dram_tensor("out", [S, D], F32, kind="ExternalOutput").ap()

    c.QT0 = dscr("QT0", [8, 128, S], BF16)
    c.KT0 = dscr("KT0", [8, 128, S], BF16)
    c.V0 = dscr("V0", [S, D], BF16)
    c.OT0 = dscr("OT0", [8, 128, S], BF16)
    c.X1 = dscr("X1", [S, D], F32)
    c.X2 = dscr("X2", [S, D], F32)
    c.QT1 = dscr("QT1", [16, 96, S], BF16)
    c.KT1 = dscr("KT1", [16, 96, S], BF16)
    c.V1 = dscr("V1", [S, D], BF16)
    c.OT1 = dscr("OT1", [8, 128, S], BF16)
    c.X3 = dscr("X3", [S, D], F32)

    ps = nc.alloc_psum_tensor("ps", [128, 4096], F32).ap()
    c.psbufs = [Buf("bank%d" % i, excl=True) for i in range(8)]
    c.ps = ps

    def bank(i, n=1):
        return V(ps[:, i * 512:(i + n) * 512], c.psbufs[i:i + n])
    c.bank = bank

    p, sb = c.p, c.sb
    idf = sb.tile([128, 128], F32, "idf")
    c.idb = sb.tile([128, 128], BF16, "idb")
    p.dma("sp", idf, c.ident)
    p.CP("dve", c.idb, idf)
    c.ones = sb.tile([128, 128], BF16, "ones")
    p.MEMSET("pool", c.ones, 1.0)
    c.eps = sb.tile([128, 1], F32, "eps")
    p.MEMSET("pool", c.eps, EPS)

    phases = debug and [d for d in debug if isinstance(d, int)] or None
    stop_after = max(phases) if phases else 99
    only = [d for d in debug if isinstance(d, str) and d.startswith("only:")]
    c.nb_limit = NB
    c.level = 99
    for d in debug:
        if isinstance(d, str) and d.startswith("lv:"):
            c.level = int(d[3:])
        if isinstance(d, str) and d.startswith("var:"):
            c.var = int(d[4:])
        if isinstance(d, str) and d.startswith("sub:"):
            c.sub = int(d[4:])
    for d in debug:
        if isinstance(d, str) and d.startswith("nb:"):
            c.nb_limit = int(d[3:])
    if only:
        which = only[0][5:]
        {"1A": phase_1A, "mla": phase_attn_mla}[which](c)
        p.barrier()
        p.emit()
        return nc

    phase_0A(c)
    p.barrier()
    m0 = sb.mark()
    wup0 = prefetch_wup(c, 0)
    phase_attn_da(c)
    p.barrier()
    wdn0 = prefetch_wdn(c, 0)
    phase_oproj(c, c.OT0, c.w_o0, c.x, c.X1)
    p.barrier()
    phase_ffn(c, 0, c.X1, c.X2, False, wup0, wdn0)
    p.barrier()
    sb.release(m0)
    phase_1A(c)
    p.barrier()
    m1 = sb.mark()
    wup1 = prefetch_wup(c, 1)
    phase_attn_mla(c)
    p.barrier()
    wdn1 = prefetch_wdn(c, 1)
    phase_oproj(c, c.OT1, c.w_o1, c.X2, c.X3)
    p.barrier()
    phase_ffn(c, 1, c.X3, c.out, True, wup1, wdn1)
    p.barrier()
    sb.release(m1)
    p.emit()
    return nc


def load_w(c, dst, src_kpn, nk, q="pool"):
    for k in range(nk):
        c.p.dma(q, dst[:, k, :], src_kpn[k * 128:(k + 1) * 128, :])


def rms_rstd(c, src, n, ss, rstd, junk):
    p = c.p
    p.A(junk, src, AF.Square, accum=ss)
    p.A(rstd, ss, AF.Sqrt, scale=1.0 / n, bias=c.eps)
    p.RECIP(rstd, rstd)


def phase_0A(c):
    p, sb, bank = c.p, c.sb, c.bank
    m = sb.mark()
    wq = sb.tile([128, 8, 3 * D], BF16, "wqkv")
    load_w(c, wq, c.w_qkv, 8)
    g0 = sb.tile([128, D], F32, "g0")
    p.dma("sp", g0, c.g_attn[0].partition_broadcast(128))
    cs = sb.tile([128, NT, 8], F32, "cos")
    sn = sb.tile([128, NT, 8], F32, "sin")
    p.dma("sp", cs, c.rope_da[0].rearrange("(t p) i -> p t i", p=128))
    p.dma("sp", sn, c.rope_da[1].rearrange("(t p) i -> p t i", p=128))
    xt = [sb.tile([128, D], F32, "xt") for _ in range(2)]
    junk = sb.tile([128, D], BF16, "junk")
    ss = [sb.tile([128, 1], F32, "ss") for _ in range(2)]
    rstd = [sb.tile([128, 1], F32, "rstd") for _ in range(2)]
    h = [sb.tile([128, D], BF16, "h") for _ in range(2)]
    hT = [sb.tile([128, 8, 128], BF16, "hT") for _ in range(2)]
    qk = [sb.tile([128, 32, 64], BF16, "qk") for _ in range(2)]
    vt = [sb.tile([128, 4, D], BF16, "vt") for _ in range(2)]
    tmp = [sb.tile([128, 32, 8], F32, "tmp") for _ in range(4)]
    qkT = [sb.tile([128, 16, 512], BF16, "qkT") for _ in range(2)]
    xv = c.x.rearrange("(t p) d -> p t d", p=128)
    p.dma("sp", xt[0], xv[:, 0, :])

    def norm_head(tq):
        j2 = tq % 2
        rms_rstd(c, xt[j2], D, ss[j2], rstd[j2], junk)
        p.STT(h[j2], xt[j2], rstd[j2], g0, ALU.mult, ALU.mult)

    for b in range(NB):
        for tt in range(4):
            t = b * 4 + tt
            i2 = t % 2
            if t + 1 < NT:
                p.dma("sp", xt[1 - i2], xv[:, t + 1, :])
            if t == 0:
                norm_head(0)
            b0 = bank(0).bitcast(BF16)
            for k in range(8):
                p.TR(b0[:, k * 128:(k + 1) * 128], h[i2][:, k * 128:(k + 1) * 128], c.idb, last=(k == 7))
            p.CP("act", hT[i2].re("p k n -> p (k n)"), b0)
            for nb_ in range(6):
                for k in range(8):
                    p.MM(bank(1 + nb_), hT[i2][:, k, :], wq[:, k, nb_ * 512:(nb_ + 1) * 512],
                         start=(k == 0), stop=(k == 7), last=(k == 7))
            if t + 1 < NT:
                norm_head(t + 1)
            qkps = bank(1, 4)
            p.CP("act", qk[i2].re("p a b -> p (a b)"), qkps)
            p.CP("dve", vt[b % 2][:, tt, :], bank(5, 2))
            q3 = qkps.re("p (a b) -> p a b", b=64)
            cb = cs[:, t, :].unsq(1).bc([128, 32, 8])
            sbb = sn[:, t, :].unsq(1).bc([128, 32, 8])
            p.TT("dve", tmp[0], q3[:, :, 0:8], cb, ALU.mult)
            p.TT("dve", tmp[1], q3[:, :, 8:16], sbb, ALU.mult)
            p.TT("dve", tmp[2], q3[:, :, 8:16], cb, ALU.mult)
            p.TT("dve", tmp[3], q3[:, :, 0:8], sbb, ALU.mult)
            p.TT("pool", qk[i2][:, :, 0:8], tmp[0], tmp[1], ALU.subtract)
            p.TT("pool", qk[i2][:, :, 8:16], tmp[2], tmp[3], ALU.add)
            qkf = qk[i2].re("p a b -> p (a b)")
            b7 = bank(7).bitcast(BF16)
            for i in range(8):
                p.TR(b7[:, i * 128:(i + 1) * 128], qkf[:, i * 128:(i + 1) * 128], c.idb, last=(i == 7))
            for i in range(8):
                p.TR(b0[:, i * 128:(i + 1) * 128], qkf[:, (8 + i) * 128:(9 + i) * 128], c.idb, last=(i == 7))
            p.CP("act", qkT[b % 2][:, 0:8, tt * 128:(tt + 1) * 128], b7.re("p (a b) -> p a b", b=128))
            p.CP("dve", qkT[b % 2][:, 8:16, tt * 128:(tt + 1) * 128], b0.re("p (a b) -> p a b", b=128))
        blk = slice(b * 512, (b + 1) * 512)
        p.dma("sp", c.QT0.rearrange("h p s -> p h s")[:, :, blk], qkT[b % 2][:, 0:8, :])
        p.dma("sp", c.KT0.rearrange("h p s -> p h s")[:, :, blk], qkT[b % 2][:, 8:16, :])
        p.dma("sp", c.V0.rearrange("(t p) d -> p t d", p=128)[:, b * 4:(b + 1) * 4, :], vt[b % 2])
    sb.release(m)


def attn_steps(qb):
    out = []
    for kt in range(4 * qb + 4):
        j = kt - 4 * qb
        if j < 0:
            out.append((kt, 0, None))
        else:
            out.append((kt, (2 * j + 1) * 64, 2 * j * 64))
    return out


def phase_attn_da(c):
    p, sb, bank = c.p, c.sb, c.bank
    m = sb.mark()
    lamt = sb.tile([128, 4, 64], F32, "lamt")
    p.dma("sp", lamt, c.lam.partition_broadcast(128))
    lprod = sb.tile([128, 2, 64], F32, "lprod")
    lsum = sb.tile([128, 2], F32, "lsum")
    neglam = sb.tile([128, 1], F32, "neglam")
    p.TT("dve", lprod[:, 0, :], lamt[:, 0, :], lamt[:, 1, :], ALU.mult)
    p.TT("dve", lprod[:, 1, :], lamt[:, 2, :], lamt[:, 3, :], ALU.mult)
    lj = sb.tile([128, 64], F32, "lj")
    p.A(lj, lprod[:, 0, :], AF.Identity, accum=lsum[:, 0:1])
    p.A(lj, lprod[:, 1, :], AF.Identity, accum=lsum[:, 1:2])
    p.A(lsum, lsum, AF.Exp)
    p.STT(neglam, lsum[:, 1:2], -LAMBDA_INIT0, lsum[:, 0:1], ALU.add, ALU.subtract)
    gsub = sb.tile([128, 1], F32, "gsub")
    p.dma("sp", gsub, c.subln)
    p.TS("dve", gsub, gsub, 1.0 - LAMBDA_INIT0, ALU.mult)

    KT = [sb.tile([128, S], BF16, "KT") for _ in range(2)]
    QTp = [[sb.tile([128, S], BF16, "QTp") for _ in range(2)] for _ in range(2)]
    for i_ in range(2):
        for cc_ in range(2):
            p.MEMSET("pool", QTp[i_][cc_], 0.0)
    Vt = [sb.tile([128, NT, 128], BF16, "Vt") for _ in range(2)]
    E = [sb.tile([128, 512], BF16, "E") for _ in range(4)]
    r0 = sb.tile([128, 512], F32, "r0")
    r1 = sb.tile([128, 512], F32, "r1")
    t0 = sb.tile([128, 512], F32, "t0")
    t1 = sb.tile([128, 512], F32, "t1")
    o = sb.tile([128, 512], F32, "o")
    sq = sb.tile([128, 512], BF16, "sq")
    rs = sb.tile([128, 512], F32, "rs")
    onb = [sb.tile([128, 512], BF16, "onb") for _ in range(2)]
    Vd = c.V0.rearrange("(t p) (h e) -> h p t e", p=128, e=128)
    scale = 64 ** -0.5
    step_ctr = [0]

    def load_head(hh):
        i2 = hh % 2
        p.dma("sp", KT[i2], c.KT0[hh])
        p.dma("sp", QTp[i2][0][0:64, :], c.QT0[hh][0:64, :])
        p.dma("sp", QTp[i2][1][64:128, :], c.QT0[hh][64:128, :])
        p.dma("sp", Vt[i2], Vd[hh])

    load_head(0)
    grp = 0
    for hh in range(8):
        if hh + 1 < 8:
            load_head(hh + 1)
        i2 = hh % 2
        for qb in range(NB):
            Ob = [bank(0), bank(1)]
            Sb = [bank(2), bank(3)]
            steps = [(cc, kt, c0, h0) for cc in range(2) for (kt, c0, h0) in attn_steps(qb)]
            n = len(steps)
            LAG = 2
            info = {}

            def score(i):
                cc, kt, c0, h0 = steps[i]
                rows = slice(cc * 64, cc * 64 + 64)
                sidx = step_ctr[0]
                step_ctr[0] += 1
                sc = bank(4 + sidx % 4)
                e = E[sidx % 4]
                info[i] = (sc, e)
                q0 = qb * 512
                p.MM(sc[:, c0:512], KT[i2][:, kt * 128:(kt + 1) * 128], QTp[i2][cc][:, q0 + c0:q0 + 512],
                     start=True, stop=True, last=(h0 is None))
                if h0 is not None:
                    p.MM(sc[0:64, h0:h0 + 64], KT[i2][:, kt * 128:kt * 128 + 64],
                         QTp[i2][cc][:, q0 + h0:q0 + h0 + 64], start=True, stop=True, last=True)
                p.A(e[:, c0:512], sc[:, c0:512], AF.Exp, scale=scale)
                if h0 is not None:
                    p.A(e[0:64, h0:h0 + 64], sc[0:64, h0:h0 + 64], AF.Exp, scale=scale)

            def pv(i):
                cc, kt, c0, h0 = steps[i]
                sc, e = info[i]
                first = (kt == 0)
                lastk = (kt == 4 * qb + 3)
                p.MM(Ob[cc][:, c0:512], Vt[i2][:, kt, :], e[:, c0:512], start=first, stop=False, last=False)
                p.MM(Sb[cc][:, c0:512], c.ones, e[:, c0:512], start=first, stop=False, last=(h0 is None))
                if h0 is not None:
                    p.MM(Ob[cc][:, h0:h0 + 64], Vt[i2][0:64, kt, :], e[0:64, h0:h0 + 64],
                         start=False, stop=lastk, last=False)
                    p.MM(Sb[cc][:, h0:h0 + 64], c.ones[0:64, :], e[0:64, h0:h0 + 64],
                         start=False, stop=lastk, last=True)

            for i in range(n + LAG):
                if i < n:
                    score(i)
                if i - LAG >= 0:
                    pv(i - LAG)
            p.A(r0, Sb[0], AF.Ln)
            p.A(r0, r0, AF.Exp, scale=-1.0)
            p.TT("dve", t0, Ob[0], r0, ALU.mult)
            p.A(r1, Sb[1], AF.Ln)
            p.A(r1, r1, AF.Exp, scale=-1.0)
            p.TT("dve", t1, Ob[1], r1, ALU.mult)
            p.STT(o, t1, neglam, t0, ALU.mult, ALU.add)
            p.TT("pool", sq, o, o, ALU.mult)
            sidx = step_ctr[0]
            step_ctr[0] += 1
            ssb = bank(4 + sidx % 4)
            p.MM(ssb, c.ones, sq, start=True, stop=True, last=True)
            p.A(rs, ssb, AF.Ln, scale=1.0 / 128, bias=c.eps)
            p.A(rs, rs, AF.Exp, scale=-0.5)
            ob = onb[grp % 2]
            grp += 1
            p.STT(ob, o, gsub, rs, ALU.mult, ALU.mult)
            p.dma("sp", c.OT0[hh][:, qb * 512:(qb + 1) * 512], ob)
    sb.release(m)


def prefetch_wo(c, w_o):
    wo = c.sb.tile([128, 8, D], BF16, "wo")
    load_w(c, wo, w_o, 8)
    return wo


def prefetch_wdn(c, layer):
    wdn = c.sb.tile([128, NJ, D], BF16, "wdn")
    load_w(c, wdn, c.w_dn[layer], NJ)
    return wdn


def phase_oproj(c, OT, w_o, Xin, Xout):
    p, sb, bank = c.p, c.sb, c.bank
    m = sb.mark()
    wo = prefetch_wo(c, w_o)
    ot = [sb.tile([128, 8, 512], BF16, "ot") for _ in range(2)]
    xin = [sb.tile([128, D], F32, "xin") for _ in range(2)]
    xo = [sb.tile([128, D], F32, "xo") for _ in range(2)]
    OTv = OT.rearrange("h p s -> p h s")
    xiv = Xin.rearrange("(t p) d -> p t d", p=128)
    xov = Xout.rearrange("(t p) d -> p t d", p=128)
    p.dma("sp", ot[0], OTv[:, :, 0:512])
    for b in range(NB):
        if b + 1 < NB:
            p.dma("sp", ot[(b + 1) % 2], OTv[:, :, (b + 1) * 512:(b + 2) * 512])
        for tt in range(4):
            t = b * 4 + tt
            i2 = t % 2
            p.dma("sp", xin[i2], xiv[:, t, :])
            pb = 2 * i2
            for nb_ in range(2):
                for hh in range(8):
                    p.MM(bank(pb + nb_), ot[b % 2][:, hh, tt * 128:(tt + 1) * 128],
                         wo[:, hh, nb_ * 512:(nb_ + 1) * 512], start=(hh == 0), stop=(hh == 7), last=(hh == 7))
            p.TT("dve", xo[i2], xin[i2], bank(pb, 2), ALU.add)
            p.dma("sp", xov[:, t, :], xo[i2])
    sb.release(m)


def prefetch_wup(c, layer):
    wup = c.sb.tile([128, 8, 2 * FF], BF16, "wup")
    load_w(c, wup, c.w_up[layer], 8)
    return wup


def phase_ffn(c, layer, Xin, Xout, final, wup, wdn):
    p, sb, bank = c.p, c.sb, c.bank
    m = sb.mark()
    TB = 256
    g = sb.tile([128, D], F32, "gffn")
    p.dma("sp", g, c.g_ffn[layer].partition_broadcast(128))
    if final:
        gf = sb.tile([128, D], F32, "gfin")
        p.dma("sp", gf, c.g_fin.partition_broadcast(128))
    cw = sb.tile([128, 3, 2 * NJ], F32, "cw")
    cbias = sb.tile([128, 2 * NJ], F32, "cb")
    for jx in range(3):
        p.dma("sp", cw[:, jx, :], c.conv_w[layer][jx])
    p.dma("sp", cbias, c.conv_b[layer])
    halo = sb.tile([128, 2 * NJ, 2], F32, "halo")
    p.MEMSET("pool", halo, 0.0)
    xt = [sb.tile([128, 2, D], F32, "xt") for _ in range(2)]
    junk = sb.tile([128, D], BF16, "junk")
    ss = sb.tile([128, 1], F32, "ss")
    rstd = sb.tile([128, 1], F32, "rstd")
    h = [sb.tile([128, D], BF16, "h") for _ in range(2)]
    hT = [sb.tile([128, 8, TB], BF16, "hT") for _ in range(2)]
    mT = [sb.tile([128, NJ, TB], BF16, "mT") for _ in range(2)]
    ub = [sb.tile([128, TB + 2], F32, "ub") for _ in range(4)]
    cg = [sb.tile([128, TB], F32, "cg") for _ in range(3)]
    cv = [sb.tile([128, TB], F32, "cv") for _ in range(3)]
    xiv = Xin.rearrange("(t p) d -> p t d", p=128)
    xov = Xout.rearrange("(t p) d -> p t d", p=128)
    nblk = S // TB
    p.dma("sp", xt[0], xiv[:, 0:2, :])
    uctr = 0
    for b in range(nblk):
        if b + 1 < nblk:
            p.dma("sp", xt[(b + 1) % 2], xiv[:, (b + 1) * 2:(b + 2) * 2, :])
        xb = xt[b % 2]
        hTb = hT[b % 2]
        for tt in range(2):
            i2 = tt
            rms_rstd(c, xb[:, tt, :], D, ss, rstd, junk)
            p.STT(h[i2], xb[:, tt, :], rstd, g, ALU.mult, ALU.mult)
            b0 = bank(0).bitcast(BF16)
            for k in range(8):
                p.TR(b0[:, k * 128:(k + 1) * 128], h[i2][:, k * 128:(k + 1) * 128], c.idb, last=(k == 7))
            p.CP("act", hTb[:, :, tt * 128:(tt + 1) * 128], b0.re("p (a b) -> p a b", b=128))
        mTb = mT[b % 2]
        pend = None
        for j in range(NJ):
            res = []
            for which in range(2):
                ch = which * NJ + j
                pb = bank(1 + (2 * j + which) % 4)
                for k in range(8):
                    p.MM(pb[:, 0:TB], wup[:, k, ch * 128:(ch + 1) * 128], hTb[:, k, :],
                         start=(k == 0), stop=(k == 7), last=(k == 7))
                u = ub[uctr % 4]
                uctr += 1
                dst = (cg if which == 0 else cv)[j % 3]
                p.A(dst, pb[:, 0:TB], AF.Identity, scale=cw[:, 2, ch:ch + 1], bias=cbias[:, ch:ch + 1])
                p.CP("act", u[:, 2:TB + 2], pb[:, 0:TB])
                p.CP("pool", u[:, 0:2], halo[:, ch, :])
                p.CP("pool", halo[:, ch, :], u[:, TB:TB + 2])
                p.STT(dst, u[:, 1:TB + 1], cw[:, 1, ch:ch + 1], dst, ALU.mult, ALU.add)
                p.STT(dst, u[:, 0:TB], cw[:, 0, ch:ch + 1], dst, ALU.mult, ALU.add)
                res.append(dst)
            if pend is not None:
                p.A(pend[0], pend[0], AF.Silu)
                p.TT("pool", pend[2], pend[0], pend[1], ALU.mult)
            pend = (res[0], res[1], mTb[:, j, :])
        p.A(pend[0], pend[0], AF.Silu)
        p.TT("pool", pend[2], pend[0], pend[1], ALU.mult)
        for tt in range(2):
            t = b * 2 + tt
            i2 = t % 2
            for nb_ in range(2):
                for j in range(NJ):
                    p.MM(bank(5 + nb_), mTb[:, j, tt * 128:(tt + 1) * 128], wdn[:, j, nb_ * 512:(nb_ + 1) * 512],
                         start=(j == 0), stop=(j == NJ - 1), last=(j == NJ - 1))
            xo_ = xb[:, tt, :]
            p.TT("dve", xo_, xo_, bank(5, 2), ALU.add)
            if final:
                rms_rstd(c, xo_, D, ss, rstd, junk)
                p.STT(xo_, xo_, rstd, gf, ALU.mult, ALU.mult)
            p.dma("sp", xov[:, t, :], xo_)
    sb.release(m)


def phase_1A(c):
    p, sb, bank = c.p, c.sb, c.bank
    m = sb.mark()
    win = sb.tile([128, 8, 672], BF16, "win")
    load_w(c, win, c.w_in, 8)
    wuq = sb.tile([128, 3, 1536], BF16, "wuq")
    load_w(c, wuq, c.w_uq, 3)
    wukv = sb.tile([128, 2, 2048], BF16, "wukv")
    load_w(c, wukv, c.w_ukv, 2)
    g1 = sb.tile([128, D], F32, "g1")
    p.dma("sp", g1, c.g_attn[1].partition_broadcast(128))
    gq = sb.tile([128, 384], F32, "gq")
    p.dma("sp", gq, c.g_q.partition_broadcast(128))
    gkv = sb.tile([128, 256], F32, "gkv")
    p.dma("sp", gkv, c.g_kv.partition_broadcast(128))
    cs = sb.tile([128, NT, 16], F32, "cos")
    sn = sb.tile([128, NT, 16], F32, "sin")
    p.dma("sp", cs, c.rope_mla[0].rearrange("(t p) i -> p t i", p=128))
    p.dma("sp", sn, c.rope_mla[1].rearrange("(t p) i -> p t i", p=128))
    xt = [sb.tile([128, D], F32, "xt") for _ in range(2)]
    junk = sb.tile([128, D], BF16, "junk")
    ss = sb.tile([128, 1], F32, "ss")
    rstd = sb.tile([128, 1], F32, "rstd")
    ss2 = sb.tile([128, 1], F32, "ss2")
    rstd2 = sb.tile([128, 1], F32, "rstd2")
    ss3 = sb.tile([128, 1], F32, "ss3")
    rstd3 = sb.tile([128, 1], F32, "rstd3")
    h = [sb.tile([128, D], BF16, "h") for _ in range(2)]
    hT = [sb.tile([128, 8, 128], BF16, "hT") for _ in range(2)]
    cqn = sb.tile([128, 384], BF16, "cqn")
    ckvn = sb.tile([128, 256], BF16, "ckvn")
    kslab = [sb.tile([128, 128], BF16, "kslab") for _ in range(2)]
    for ks in kslab:
        p.MEMSET("pool", ks, 0.0)
    kt4 = [sb.tile([128, 16], F32, "kt4") for _ in range(4)]
    cqT = [sb.tile([128, 3, 128], BF16, "cqT") for _ in range(2)]
    ckvT = [sb.tile([128, 2, 512], BF16, "ckvT") for _ in range(2)]
    kpeT = [sb.tile([128, 512], BF16, "kpeT") for _ in range(2)]
    qtok = [sb.tile([128, 16, 128], BF16, "qtok") for _ in range(2)]
    for qq in qtok:
        p.MEMSET("pool", qq, 0.0)
    qt4 = [sb.tile([128, 16, 16], F32, "qt4") for _ in range(4)]
    qT = [sb.tile([128, 16, 512], BF16, "qT") for _ in range(2)]
    vtok = [sb.tile([128, 4, D], BF16, "vtok") for _ in range(2)]
    knb = [sb.tile([128, 512], BF16, "knb") for _ in range(4)]
    xv = c.X2.rearrange("(t p) d -> p t d", p=128)
    kctr = 0
    nblk_ = min(NB, c.nb_limit)
    p.dma("sp", xt[0], xv[:, 0, :])
    for b in range(nblk_):
        bb = b % 2
        for tt in range(4):
            t = b * 4 + tt
            i2 = t % 2
            if t + 1 < 4 * nblk_:
                p.dma("sp", xt[1 - i2], xv[:, t + 1, :])
            rms_rstd(c, xt[i2], D, ss, rstd, junk)
            p.STT(h[i2], xt[i2], rstd, g1, ALU.mult, ALU.mult)
            b0 = bank(0).bitcast(BF16)
            for k in range(8):
                p.TR(b0[:, k * 128:(k + 1) * 128], h[i2][:, k * 128:(k + 1) * 128], c.idb, last=(k == 7))
            p.CP("act", hT[i2].re("p k n -> p (k n)"), b0)
            if c.level >= 2:
                b1, b2 = bank(1), bank(2)
                for k in range(8):
                    p.MM(b1[:, 0:384], hT[i2][:, k, :], win[:, k, 0:384], start=(k == 0), stop=(k == 7), last=(k == 7))
                for k in range(8):
                    p.MM(b2[:, 0:288], hT[i2][:, k, :], win[:, k, 384:672], start=(k == 0), stop=(k == 7), last=(k == 7))
                rms_rstd(c, b1[:, 0:384], 384, ss2, rstd2, junk[:, 0:384])
                p.STT(cqn, b1[:, 0:384], rstd2, gq, ALU.mult, ALU.mult)
                rms_rstd(c, b2[:, 0:256], 256, ss3, rstd3, junk[:, 0:256])
                p.STT(ckvn, b2[:, 0:256], rstd3, gkv, ALU.mult, ALU.mult)
            if c.level >= 3:
                ks = kslab[i2]
                p.TT("dve", kt4[0], b2[:, 256:272], cs[:, t, :], ALU.mult)
                p.TT("dve", kt4[1], b2[:, 272:288], sn[:, t, :], ALU.mult)
                p.TT("dve", kt4[2], b2[:, 272:288], cs[:, t, :], ALU.mult)
                p.TT("dve", kt4[3], b2[:, 256:272], sn[:, t, :], ALU.mult)
                p.TT("pool", ks[:, 64:80], kt4[0], kt4[1], ALU.subtract)
                p.TT("pool", ks[:, 80:96], kt4[2], kt4[3], ALU.add)
            if c.level >= 4:
                b3 = bank(3).bitcast(BF16)
                sub = getattr(c, "sub", 9)
                for k in range(3):
                    p.TR(b3[:, k * 128:(k + 1) * 128], cqn[:, k * 128:(k + 1) * 128], c.idb, last=(sub == 1 and k == 2))
                if sub >= 2:
                    for k in range(2):
                        p.TR(b3[:, 384 + k * 128:384 + (k + 1) * 128], ckvn[:, k * 128:(k + 1) * 128], c.idb,
                             last=(sub == 2 and k == 1))
                if sub >= 3:
                    p.TR(b3[:, 640:768], ks, c.idb, last=True)
                p.CP("act", cqT[i2].re("p k n -> p (k n)"), b3[:, 0:384])
                var = getattr(c, "var", 0)
                if sub >= 2 and var != 1:
                    p.CP("act", ckvT[bb][:, :, tt * 128:(tt + 1) * 128], b3[:, 384:640].re("p (a b) -> p a b", b=128))
                if sub >= 3:
                    p.CP("dve", kpeT[bb][:, tt * 128:(tt + 1) * 128], b3[:, 640:768])
            if c.level >= 5:
                for nb_ in range(3):
                    for k in range(3):
                        p.MM(bank(4 + nb_), cqT[i2][:, k, :], wuq[:, k, nb_ * 512:(nb_ + 1) * 512],
                             start=(k == 0), stop=(k == 2), last=(k == 2))
                qps = bank(4, 3)
                p.CP("act", qtok[i2][:, :, 0:96], qps.re("p (a b) -> p a b", b=96))
                q3 = qps.re("p (a b) -> p a b", b=96)
                cb = cs[:, t, :].unsq(1).bc([128, 16, 16])
                sbb = sn[:, t, :].unsq(1).bc([128, 16, 16])
                p.TT("dve", qt4[0], q3[:, :, 64:80], cb, ALU.mult)
                p.TT("dve", qt4[1], q3[:, :, 80:96], sbb, ALU.mult)
                p.TT("dve", qt4[2], q3[:, :, 80:96], cb, ALU.mult)
                p.TT("dve", qt4[3], q3[:, :, 64:80], sbb, ALU.mult)
                p.TT("pool", qtok[i2][:, :, 64:80], qt4[0], qt4[1], ALU.subtract)
                p.TT("pool", qtok[i2][:, :, 80:96], qt4[2], qt4[3], ALU.add)
            if c.level >= 6:
                for nb_ in range(2):
                    for k in range(2):
                        p.MM(bank(1 + nb_), ckvT[bb][:, k, tt * 128:(tt + 1) * 128],
                             wukv[:, k, 1024 + nb_ * 512:1024 + (nb_ + 1) * 512],
                             start=(k == 0), stop=(k == 1), last=(k == 1))
                p.CP("act", vtok[bb][:, tt, :], bank(1, 2))
            if c.level >= 7:
                b7 = bank(7).bitcast(BF16)
                for hh in range(8):
                    p.TR(b7[:, hh * 128:(hh + 1) * 128], qtok[i2][:, hh, :], c.idb, last=(hh == 7))
                for hh in range(8):
                    p.TR(b0[:, hh * 128:(hh + 1) * 128], qtok[i2][:, 8 + hh, :], c.idb, last=(hh == 7))
                p.CP("act", qT[bb][:, 0:8, tt * 128:(tt + 1) * 128], b7.re("p (a b) -> p a b", b=128))
                p.CP("dve", qT[bb][:, 8:16, tt * 128:(tt + 1) * 128], b0.re("p (a b) -> p a b", b=128))
        blk = slice(b * 512, (b + 1) * 512)
        if c.level >= 8:
            for pr in range(8):
                pb = bank(3 + pr % 4)
                for k in range(2):
                    p.MM(pb, wukv[:, k, pr * 128:(pr + 1) * 128], ckvT[bb][:, k, :], start=(k == 0), stop=(k == 1),
                         last=(k == 1))
                kn = knb[kctr % 4]
                kctr += 1
                p.CP("dve" if pr % 2 else "act", kn, pb)
                p.dma("sp", c.KT1[2 * pr][0:64, blk], kn[0:64, :])
                p.dma("sp", c.KT1[2 * pr + 1][0:64, blk], kn[64:128, :])
        if c.level >= 9:
            for hh in range(16):
                p.dma("sp", c.KT1[hh][64:96, blk], kpeT[bb][64:96, :])
        if c.level >= 10:
            p.dma("sp", c.QT1.rearrange("h p s -> p h s")[:, :, blk], qT[bb][0:96, :, :])
            p.dma("sp", c.V1.rearrange("(t p) d -> p t d", p=128)[:, b * 4:(b + 1) * 4, :], vtok[bb])
    sb.release(m)


def phase_attn_mla(c):
    p, sb, bank = c.p, c.sb, c.bank
    m = sb.mark()
    KT = [sb.tile([128, S], BF16, "KT") for _ in range(2)]
    QT = [sb.tile([128, S], BF16, "QT") for _ in range(2)]
    VA = [sb.tile([128, NT, 128], BF16, "VA") for _ in range(2)]
    for va in VA:
        p.MEMSET("pool", va[:, :, 64:128], 1.0)
    E = [sb.tile([128, 512], BF16, "E") for _ in range(6)]
    r = sb.tile([128, 512], F32, "r")
    onb = [sb.tile([128, 512], BF16, "onb") for _ in range(2)]
    Vd = c.V1.rearrange("(t p) (h e) -> h p t e", p=128, e=64)
    OTv = c.OT1.rearrange("a (b e) s -> (a b) e s", e=64)
    scale = 96 ** -0.5
    ctr = 0

    def load_head(hh):
        i2 = hh % 2
        p.dma("sp", KT[i2][0:96, :], c.KT1[hh])
        p.dma("sp", QT[i2][0:96, :], c.QT1[hh])
        p.dma("sp", VA[i2][:, :, 0:64], Vd[hh])

    load_head(0)
    grp = 0
    rows = slice(0, 96)
    for hh in range(16):
        if hh + 1 < 16:
            load_head(hh + 1)
        i2 = hh % 2
        for qb in range(NB):
            Ob = bank(grp % 2)
            steps = attn_steps(qb)
            n = len(steps)
            LAG = 3
            info = {}
            q0 = qb * 512

            def score(i):
                nonlocal ctr
                kt, c0, h0 = steps[i]
                sc = bank(2 + ctr % 6)
                e = E[ctr % 6]
                ctr += 1
                info[i] = (sc, e)
                p.MM(sc[:, c0:512], KT[i2][rows, kt * 128:(kt + 1) * 128], QT[i2][rows, q0 + c0:q0 + 512],
                     start=True, stop=True, last=(h0 is None))
                if h0 is not None:
                    p.MM(sc[0:64, h0:h0 + 64], KT[i2][rows, kt * 128:kt * 128 + 64],
                         QT[i2][rows, q0 + h0:q0 + h0 + 64], start=True, stop=True, last=True)
                p.A(e[:, c0:512], sc[:, c0:512], AF.Exp, scale=scale)
                if h0 is not None:
                    p.A(e[0:64, h0:h0 + 64], sc[0:64, h0:h0 + 64], AF.Exp, scale=scale)

            def pv(i):
                kt, c0, h0 = steps[i]
                sc, e = info[i]
                first = (kt == 0)
                lastk = (kt == 4 * qb + 3)
                p.MM(Ob[:, c0:512], VA[i2][:, kt, :], e[:, c0:512], start=first, stop=False, last=(h0 is None))
                if h0 is not None:
                    p.MM(Ob[:, h0:h0 + 64], VA[i2][0:64, kt, :], e[0:64, h0:h0 + 64],
                         start=False, stop=lastk, last=True)

            for i in range(n + LAG):
                if i < n:
                    score(i)
                if i - LAG >= 0:
                    pv(i - LAG)
            p.RECIP(r[0:64, :], Ob[64:128, :])
            ob = onb[grp % 2]
            grp += 1
            p.TT("dve", ob[0:64, :], Ob[0:64, :], r[0:64, :], ALU.mult)
            p.dma("sp", OTv[hh][:, qb * 512:(qb + 1) * 512], ob[0:64, :])
    sb.release(m)


def rope_tables(rot):
    inv = (np.float32(500000.0) ** (-np.arange(0, rot, 2, dtype=np.float32) / np.float32(rot))).astype(np.float32)
    ang = (np.arange(S, dtype=np.float32)[:, None] * inv[None, :]).astype(np.float32)
    return np.stack([np.cos(ang), np.sin(ang)]).astype(np.float32)


def make_in_maps(inp):
    f = lambda a: np.ascontiguousarray(np.asarray(a, dtype=np.float32))
    ukv = f(inp["mla_w_ukv"][0]).reshape(256, 16, 128)
    ukv = np.ascontiguousarray(np.concatenate([ukv[:, :, :64].reshape(256, 1024),
                                               ukv[:, :, 64:].reshape(256, 1024)], axis=1))
    shared = {
        "attn_norm_g": f(inp["attn_norm_g"]),
        "ffn_norm_g": f(inp["ffn_norm_g"]),
        "da_w_qkv": f(inp["da_w_qkv"][0]),
        "da_lam": f(np.stack([inp["da_lam_q1"][0], inp["da_lam_k1"][0], inp["da_lam_q2"][0], inp["da_lam_k2"][0]])),
        "da_subln_g": f(inp["da_subln_g"][0]).reshape(128, 1),
        "da_w_o": f(inp["da_w_o"][0]),
        "mla_w_in": f(inp["mla_w_in"][0]),
        "mla_q_norm_g": f(inp["mla_q_norm_g"][0]),
        "mla_kv_norm_g": f(inp["mla_kv_norm_g"][0]),
        "mla_w_uq": f(inp["mla_w_uq"][0]),
        "mla_w_ukv": ukv,
        "mla_w_o": f(inp["mla_w_o"][0]),
        "ffn_w_up": f(inp["ffn_w_up"]),
        "ffn_conv_w": f(np.asarray(inp["ffn_conv_w"]).reshape(2, 3, 2 * NJ, 128).transpose(0, 1, 3, 2)),
        "ffn_conv_b": f(np.asarray(inp["ffn_conv_b"]).reshape(2, 2 * NJ, 128).transpose(0, 2, 1)),
        "ffn_w_down": f(inp["ffn_w_down"]),
        "final_norm_g": f(inp["final_norm_g"]),
        "ident": np.eye(128, dtype=np.float32),
        "rope_da": rope_tables(16),
        "rope_mla": rope_tables(32),
    }
    x = np.asarray(inp["x"], dtype=np.float32)
    maps = []
    for i in range(8):
        d = dict(shared)
        d["x"] = np.ascontiguousarray(x[i])
        maps.append(d)
    return maps


_NC_CACHE = {}


def kernel(**inputs):
    if "nc" not in _NC_CACHE:
        _NC_CACHE["nc"] = build_program()
    nc = _NC_CACHE["nc"]
    res = run_bass_kernel_spmd(nc, make_in_maps(inputs), core_ids=list(range(8)))
    return np.stack([np.asarray(r["out"], dtype=np.float32) for r in res.results], axis=0)
```
